# Optimizing a Trainium2 kernel written in Bass

```python
import jax, jax.numpy as jnp
from jax import lax
import numpy as np

D_MODEL = 1024
BATCH = 32
SEQ = 2048
DEPTH = 2

HEAD_DIM = 64
N_ATTN_HEADS = 8
N_KV_HEADS = 2
GQA_GROUP = N_ATTN_HEADS // N_KV_HEADS
ATTN_WIDTH = N_ATTN_HEADS * HEAD_DIM
KV_WIDTH = N_KV_HEADS * HEAD_DIM
ROPE_DIM = HEAD_DIM // 4
ROPE_THETA = 500000.0
DILATED_PATTERNS = ((128, 1), (512, 4), (2048, 16))
ATTN_BLOCK = 128

SSM_HEAD_DIM = 64
SSM_HEADS = 16
SSM_INNER = SSM_HEADS * SSM_HEAD_DIM
SSM_GROUPS = 2
D_STATE = 128
CONV_WIDTH = 4
CHUNK = 128
CONV_CH = SSM_INNER + 2 * SSM_GROUPS * D_STATE

MIX_WIDTH = ATTN_WIDTH + SSM_INNER
Q_END = ATTN_WIDTH
K_END = Q_END + KV_WIDTH
V_END = K_END + KV_WIDTH
Z_END = V_END + SSM_INNER
XBC_END = Z_END + CONV_CH
IN_PROJ = XBC_END + SSM_HEADS

FFN_HIDDEN = ((8 * D_MODEL + 3 * 256 - 1) // (3 * 256)) * 256
EPS = 1e-5

kernel_name = "hybrid_dilated_attn_mamba2_block"


def rmsnorm(x, w):
    xf = x.astype(jnp.float32)
    y = xf * lax.rsqrt(jnp.mean(xf * xf, axis=-1, keepdims=True) + EPS)
    return (y * w.astype(jnp.float32)).astype(x.dtype)


def rotary_tables(positions, dtype):
    inv_freq = ROPE_THETA ** (-jnp.arange(0, ROPE_DIM, 2, dtype=jnp.float32) / ROPE_DIM)
    ang = positions.astype(jnp.float32)[..., None] * inv_freq
    return jnp.cos(ang)[:, :, None, :].astype(dtype), jnp.sin(ang)[:, :, None, :].astype(dtype)


def partial_rotary(t, cos, sin):
    half = ROPE_DIM // 2
    t1, t2, rest = t[..., :half], t[..., half:ROPE_DIM], t[..., ROPE_DIM:]
    return jnp.concatenate([t1 * cos - t2 * sin, t2 * cos + t1 * sin, rest], axis=-1)


def dilated_window_branch(q, k, v, window, dilation):
    bsz, s = q.shape[0], q.shape[1]
    length = s // dilation
    w_d = window // dilation
    nb = -(-length // ATTN_BLOCK)
    lp = nb * ATTN_BLOCK
    qd = q.reshape((bsz, length, dilation) + q.shape[2:])
    kd = k.reshape((bsz, length, dilation) + k.shape[2:])
    vd = v.reshape((bsz, length, dilation) + v.shape[2:])
    qd = jnp.pad(qd, [(0, 0), (0, lp - length)] + [(0, 0)] * (qd.ndim - 2))
    kv_pad = [(0, 0), (ATTN_BLOCK, lp - length)] + [(0, 0)] * (kd.ndim - 2)
    kd = jnp.pad(kd, kv_pad)
    vd = jnp.pad(vd, kv_pad)
    qb = qd.reshape((bsz, nb, ATTN_BLOCK) + qd.shape[2:])
    kb = kd.reshape((bsz, nb + 1, ATTN_BLOCK) + kd.shape[2:])
    vb = vd.reshape((bsz, nb + 1, ATTN_BLOCK) + vd.shape[2:])
    kb = jnp.concatenate([kb[:, :-1], kb[:, 1:]], axis=2)
    vb = jnp.concatenate([vb[:, :-1], vb[:, 1:]], axis=2)
    scores = jnp.einsum('bnqrhgc,bnkrhc->bnrhgqk', qb, kb).astype(jnp.float32)
    qi = jnp.arange(ATTN_BLOCK)[:, None]
    ki = jnp.arange(2 * ATTN_BLOCK)[None, :]
    delta = qi + ATTN_BLOCK - ki
    kpos = jnp.arange(nb)[:, None, None] * ATTN_BLOCK - ATTN_BLOCK + ki[None]
    valid = (delta >= 0)[None] & (delta <= w_d)[None] & (kpos >= 0)
    scores = jnp.where(valid[None, :, None, None, None], scores, -jnp.inf)
    m = jnp.max(scores, axis=-1, keepdims=True)
    p = jnp.exp(scores - m)
    den = jnp.sum(p, axis=-1, keepdims=True)
    o = jnp.einsum('bnrhgqk,bnkrhc->bnrhgqc', p, vb.astype(jnp.float32)) / den
    lse = (m + jnp.log(den))[..., 0]
    o = jnp.transpose(o, (0, 1, 5, 2, 3, 4, 6)).reshape((bsz, lp, dilation) + q.shape[2:])
    lse = jnp.transpose(lse, (0, 1, 5, 2, 3, 4)).reshape((bsz, lp, dilation) + q.shape[2:4])
    o = o[:, :length].reshape(q.shape)
    lse = lse[:, :length].reshape(q.shape[:4])
    return o, lse


def dilated_attention(q, k, v):
    outs, lses = [], []
    for window, dilation in DILATED_PATTERNS:
        o, l = dilated_window_branch(q, k, v, window, dilation)
        outs.append(o)
        lses.append(l)
    wts = jax.nn.softmax(jnp.stack(lses, axis=0), axis=0)
    return jnp.einsum('ibshg,ibshgc->bshgc', wts, jnp.stack(outs, axis=0))


def causal_depthwise_conv(u, w, b):
    y = lax.conv_general_dilated(u, w[:, None, :], window_strides=(1,),
                                 padding=[(CONV_WIDTH - 1, 0)],
                                 dimension_numbers=('NWC', 'WIO', 'NWC'),
                                 feature_group_count=u.shape[-1])
    return y + b


def segsum_exp(a):
    cs = jnp.cumsum(a, axis=-1)
    diff = cs[..., :, None] - cs[..., None, :]
    t = a.shape[-1]
    mask = jnp.tril(jnp.ones((t, t), dtype=bool))
    return jnp.exp(jnp.where(mask, diff, -jnp.inf))


def ssd_chunked(xs, dt, a_neg, bm, cm):
    bsz, s, nh, hp = xs.shape
    nc = s // CHUNK
    e = nh // SSM_GROUPS
    xg = (xs.astype(jnp.float32) * dt[..., None]).reshape(bsz, nc, CHUNK, SSM_GROUPS, e, hp)
    a = jnp.transpose((dt * a_neg).reshape(bsz, nc, CHUNK, SSM_GROUPS, e), (0, 1, 3, 4, 2))
    bc = bm.astype(jnp.float32).reshape(bsz, nc, CHUNK, SSM_GROUPS, D_STATE)
    cc = cm.astype(jnp.float32).reshape(bsz, nc, CHUNK, SSM_GROUPS, D_STATE)
    a_cs = jnp.cumsum(a, axis=-1)
    cb = jnp.einsum('bclgn,bcsgn->bcgls', cc, bc)
    m_mat = cb[:, :, :, None] * segsum_exp(a)
    y_diag = jnp.einsum('bcgels,bcsgep->bclgep', m_mat, xg)
    decay_states = jnp.exp(a_cs[..., -1:] - a_cs)
    states = jnp.einsum('bclgn,bcgel,bclgep->bcgepn', bc, decay_states, xg)
    chunk_decay = jnp.exp(a_cs[..., -1])

    def step(h, inp):
        dec, st = inp
        return h * dec[..., None, None] + st, h

    h0 = jnp.zeros((bsz, SSM_GROUPS, e, hp, D_STATE), jnp.float32)
    _, prev = lax.scan(step, h0, (jnp.moveaxis(chunk_decay, 1, 0), jnp.moveaxis(states, 1, 0)))
    prev_states = jnp.moveaxis(prev, 0, 1)
    y_off = jnp.einsum('bclgn,bcgepn,bcgel->bclgep', cc, prev_states, jnp.exp(a_cs))
    return (y_diag + y_off).reshape(bsz, s, nh, hp)


def hybrid_mixer(h, w_in, conv_w, conv_b, dt_bias, a_log, d_skip, ssm_norm, w_out, cos, sin):
    bsz, s, _ = h.shape
    proj = h @ w_in
    q, k, v, z, xbc, dt = jnp.split(proj, [Q_END, K_END, V_END, Z_END, XBC_END], axis=-1)
    q = partial_rotary(q.reshape(bsz, s, N_ATTN_HEADS, HEAD_DIM), cos, sin)
    k = partial_rotary(k.reshape(bsz, s, N_KV_HEADS, HEAD_DIM), cos, sin)
    v = v.reshape(bsz, s, N_KV_HEADS, HEAD_DIM)
    q = (q * (HEAD_DIM ** -0.5)).reshape(bsz, s, N_KV_HEADS, GQA_GROUP, HEAD_DIM)
    attn = dilated_attention(q, k, v).reshape(bsz, s, ATTN_WIDTH).astype(h.dtype)
    xbc = jax.nn.silu(causal_depthwise_conv(xbc, conv_w, conv_b))
    xs, bm, cm = jnp.split(xbc, [SSM_INNER, SSM_INNER + SSM_GROUPS * D_STATE], axis=-1)
    xs = xs.reshape(bsz, s, SSM_HEADS, SSM_HEAD_DIM)
    bm = bm.reshape(bsz, s, SSM_GROUPS, D_STATE)
    cm = cm.reshape(bsz, s, SSM_GROUPS, D_STATE)
    dt = jax.nn.softplus(dt.astype(jnp.float32) + dt_bias.astype(jnp.float32))
    a_neg = -jnp.exp(a_log.astype(jnp.float32))
    y = ssd_chunked(xs, dt, a_neg, bm, cm) + d_skip.astype(jnp.float32)[:, None] * xs.astype(jnp.float32)
    y = y.reshape(bsz, s, SSM_INNER).astype(h.dtype) * jax.nn.silu(z)
    gsize = SSM_INNER // SSM_GROUPS
    y = rmsnorm(y.reshape(bsz, s, SSM_GROUPS, gsize), ssm_norm.reshape(SSM_GROUPS, gsize))
    y = y.reshape(bsz, s, SSM_INNER)
    return jnp.concatenate([attn, y], axis=-1) @ w_out


def swiglu(h, w_gate, w_up, w_down):
    return (jax.nn.silu(h @ w_gate) * (h @ w_up)) @ w_down


def setup_inputs(seed: int = 0) -> dict:
    key = jax.random.key(seed)
    ks = jax.random.split(key, 16)
    f32 = jnp.float32
    x = jax.random.normal(ks[0], (BATCH, SEQ, D_MODEL), f32)
    positions = jnp.broadcast_to(jnp.arange(SEQ, dtype=jnp.int32), (BATCH, SEQ))
    norm_mix = 1.0 + 0.02 * jax.random.normal(ks[1], (DEPTH, D_MODEL), f32)
    w_in = jax.random.normal(ks[2], (DEPTH, D_MODEL, IN_PROJ), f32) * D_MODEL ** -0.5
    conv_w = jax.random.normal(ks[3], (DEPTH, CONV_WIDTH, CONV_CH), f32) * CONV_WIDTH ** -0.5
    conv_b = 0.01 * jax.random.normal(ks[4], (DEPTH, CONV_CH), f32)
    dt0 = jnp.exp(jax.random.uniform(ks[5], (DEPTH, SSM_HEADS), f32, np.log(1e-3), np.log(1e-1)))
    dt_bias = dt0 + jnp.log(-jnp.expm1(-dt0))
    a_log = jnp.log(jax.random.uniform(ks[6], (DEPTH, SSM_HEADS), f32, 1.0, 16.0))
    d_skip = 1.0 + 0.1 * jax.random.normal(ks[7], (DEPTH, SSM_HEADS), f32)
    ssm_norm = 1.0 + 0.02 * jax.random.normal(ks[8], (DEPTH, SSM_INNER), f32)
    w_out = jax.random.normal(ks[9], (DEPTH, MIX_WIDTH, D_MODEL), f32) * MIX_WIDTH ** -0.5
    norm_ffn = 1.0 + 0.02 * jax.random.normal(ks[10], (DEPTH, D_MODEL), f32)
    w_gate = jax.random.normal(ks[11], (DEPTH, D_MODEL, FFN_HIDDEN), f32) * D_MODEL ** -0.5
    w_up = jax.random.normal(ks[12], (DEPTH, D_MODEL, FFN_HIDDEN), f32) * D_MODEL ** -0.5
    w_down = jax.random.normal(ks[13], (DEPTH, FFN_HIDDEN, D_MODEL), f32) * FFN_HIDDEN ** -0.5
    final_norm = 1.0 + 0.02 * jax.random.normal(ks[14], (D_MODEL,), f32)
    return {"x": x, "positions": positions, "norm_mix": norm_mix, "w_in": w_in,
            "conv_w": conv_w, "conv_b": conv_b, "dt_bias": dt_bias, "a_log": a_log,
            "d_skip": d_skip, "ssm_norm": ssm_norm, "w_out": w_out, "norm_ffn": norm_ffn,
            "w_gate": w_gate, "w_up": w_up, "w_down": w_down, "final_norm": final_norm}


def reference(x, positions, norm_mix, w_in, conv_w, conv_b, dt_bias, a_log, d_skip, ssm_norm,
              w_out, norm_ffn, w_gate, w_up, w_down, final_norm):
    cos, sin = rotary_tables(positions, x.dtype)
    h = x
    for layer in range(DEPTH):
        h = h + hybrid_mixer(rmsnorm(h, norm_mix[layer]), w_in[layer], conv_w[layer],
                             conv_b[layer], dt_bias[layer], a_log[layer], d_skip[layer],
                             ssm_norm[layer], w_out[layer], cos, sin)
        h = h + swiglu(rmsnorm(h, norm_ffn[layer]), w_gate[layer], w_up[layer], w_down[layer])
    return rmsnorm(h, final_norm)
```

```python
import math
import numpy as np
import concourse.bass as bass
import concourse.mybir as mybir
from concourse.bass_utils import run_bass_kernel_spmd

F32 = mybir.dt.float32
BF16 = mybir.dt.bfloat16
I32 = mybir.dt.int32
AF = mybir.ActivationFunctionType
ALU = mybir.AluOpType

D = 1024
T = 2048
NT = 16
KC = 8
NL = 2
FF = 2816
EPS = 1e-5
N_CORES = 8
TWO_PI = 2.0 * math.pi


class Trk:
    __slots__ = ("w", "r")

    def __init__(self):
        self.w = None
        self.r = {}


class Sched:
    def __init__(self, nc):
        self.nc = nc
        self.eng = {"pe": nc.tensor, "act": nc.scalar, "dve": nc.vector, "pool": nc.gpsimd, "sp": nc.sync}
        self.sem = {}
        self.cnt = {}
        self.seen = {e: {} for e in self.eng}
        for e in ("pe", "act", "dve", "pool"):
            self.sem[e] = nc.alloc_semaphore("s_" + e)
            self.cnt[e] = 0

    def dsem(self, name):
        if name not in self.sem:
            self.sem[name] = self.nc.alloc_semaphore("s_" + name)
            self.cnt[name] = 0
        return name

    def _wait(self, e, reads, writes):
        deps = {}
        for t in reads:
            if t.w is not None:
                deps[t.w[0]] = max(deps.get(t.w[0], 0), t.w[1])
        for t in writes:
            if t.w is not None:
                deps[t.w[0]] = max(deps.get(t.w[0], 0), t.w[1])
            for f, c in t.r.items():
                deps[f] = max(deps.get(f, 0), c)
        for f, c in deps.items():
            if f == e and e == "pe":
                continue
            if f not in self.eng:
                c = self.cnt[f]
            if c > self.seen[e].get(f, 0):
                self.eng[e].wait_ge(self.sem[f], c)
                self.seen[e][f] = c

    def op(self, e, fn, reads=(), writes=(), inc=True):
        self._wait(e, reads, writes)
        ins = fn()
        if inc:
            ins.then_inc(self.sem[e], 1)
            self.cnt[e] += 1
            mark = self.cnt[e]
        else:
            mark = self.cnt[e] + 1
        for t in reads:
            t.r[e] = max(t.r.get(e, 0), mark)
        for t in writes:
            t.w = (e, mark)
            t.r = {}
        return ins

    def dma(self, q, ds, out, in_, reads=(), writes=(), **kw):
        self.dsem(ds)
        self._wait(q, reads, writes)
        ins = self.eng[q].dma_start(out=out, in_=in_, **kw)
        ins.then_inc(self.sem[ds], 16)
        self.cnt[ds] += 16
        mark = self.cnt[ds]
        for t in reads:
            t.r[ds] = max(t.r.get(ds, 0), mark)
        for t in writes:
            t.w = (ds, mark)
            t.r = {}
        return ins

    def barrier(self):
        for e in self.eng:
            for f, c in self.cnt.items():
                if f == e and e == "pe":
                    continue
                if c > self.seen[e].get(f, 0):
                    self.eng[e].wait_ge(self.sem[f], c)
                    self.seen[e][f] = c

    def finish(self, e, trks):
        self._wait(e, (), trks)


def build(nseq=4, dump=None, skip=()):
    nc = bass.Bass("TRN2", target_bir_lowering=False)
    S = Sched(nc)

    def din(name, shape, dt=F32):
        return nc.dram_tensor(name, list(shape), dt, kind="ExternalInput").ap()

    x_d = din("x", [nseq, T, D])
    pos_d = din("pos", [nseq, T], I32)
    w_in_d = din("w_in", [NL, D, 3344])
    w_out_d = din("w_out", [NL, 1536, D])
    w_gate_d = din("w_gate", [NL, D, FF])
    w_up_d = din("w_up", [NL, D, FF])
    w_down_d = din("w_down", [NL, FF, D])
    conv_w_d = din("conv_w", [NL, 4, 1536])
    conv_b_d = din("conv_b", [NL, 1536])
    dt_bias_d = din("dt_bias", [NL, 16])
    a_log_d = din("a_log", [NL, 16])
    d_skip_d = din("d_skip", [NL, 16])
    ssm_norm_d = din("ssm_norm", [NL, 1024])
    norms_d = din("norms", [5, 1024])
    out_d = nc.dram_tensor("out", [nseq, T, D], F32, kind="ExternalOutput").ap()

    dbg_outs = {}

    def sb(name, shape, dt):
        return nc.alloc_sbuf_tensor(name, list(shape), dt)

    PS = nc.alloc_psum_tensor("PS", [128, 6, 512], F32)
    PB = nc.alloc_psum_tensor("PB", [128, 2, 1024], BF16)
    PT = [Trk() for _ in range(6)]
    PBT = [Trk(), Trk()]

    hT = sb("hT", [128, KC, T], F32)
    hT_t = [Trk() for _ in range(4)]
    hnT = sb("hnT", [128, KC, T], BF16)
    hnT_t = [Trk() for _ in range(4)]

    ones_bf = sb("ones_bf", [128, 128], BF16)
    zeros_bf = sb("zeros_bf", [128, 128], BF16)
    e0_bf = sb("e0_bf", [128, 128], BF16)
    ident_bf = sb("ident_bf", [128, 128], BF16)
    ident_f = sb("ident_f", [128, 128], F32)
    ones_f = sb("ones_f", [128, 128], F32)
    tri_f = sb("tri_f", [128, 128], F32)
    mdiag = sb("mdiag", [128, 128], BF16)
    mprev = sb("mprev", [128, 128], BF16)
    m3 = sb("m3", [128, 4, 32], BF16)
    invf = sb("invf", [128, 8], F32)
    pcol = sb("pcol", [128, 12, 10], F32)
    ncol = sb("ncol", [128, 8, 5], F32)
    bc16 = sb("bc16", [128, 3, NL * 16], F32)
    epsc = sb("epsc", [128, 2], F32)
    cosb = sb("cosb", [128, 16, 8], F32)
    sinb = sb("sinb", [128, 16, 8], F32)
    rope_t = Trk()
    NEGm = sb("NEGm", [128, 128], BF16)
    C = Trk()

    arena_base = (nc.sbuf_base + 63) // 64 * 64
    arena_size = nc.sbuf_top - arena_base - 64
    nc.alloc_sbuf_tensor("arena", [128, (nc.sbuf_top - nc.sbuf_base - 64) // 4], F32)
    ar = {"off": 0}
    DTS = {F32: 4, BF16: 2, I32: 4}

    def region():
        ar["off"] = 0

    def al(name, shape, dt, at=None):
        n = DTS[dt]
        for d in shape[1:]:
            n *= d
        if at is None:
            off = (ar["off"] + 31) // 32 * 32
            ar["off"] = off + n
        else:
            off = at
        assert off + n <= arena_size, (name, off, n, arena_size)
        return nc.alloc_sbuf_tensor_at(name, list(shape), dt, offset=arena_base + off)

    region()
    sq = [al("sq0", [128, KC, 512], BF16)]
    sq_t = [Trk()]
    rstd = al("rstd", [128, 512], F32)
    rstd_t = Trk()
    norm_end = ar["off"]
    wq = al("wq", [128, KC, 768], BF16)
    wq_t = Trk()
    wo_a = al("wo_a", [128, 4, D], BF16)
    wo_a_t = Trk()
    qT = al("qT", [128, 4, T], BF16)
    qT_t = [Trk() for _ in range(NT)]
    kT = al("kT", [128, T], BF16)
    kT_t = [Trk() for _ in range(NT)]
    va_off = (ar["off"] + 31) // 32 * 32
    vaug = [al("vaug%d" % i, [128, 16, 2, 128], BF16) for i in range(3)]
    vaug_t = [[Trk() for _ in range(16)] for _ in range(3)]
    prow = al("prow", [10, 1536], F32, at=va_off)
    nrow = al("nrow", [5, 1024], F32, at=va_off + 6144)
    qk_f = al("qk_f", [128, 640], F32)
    qk_f_t = Trk()
    rtmp = al("rtmp", [128, 4, 10, 8], F32)
    rtmp_t = Trk()
    qkb = al("qkb", [128, 640], BF16)
    qkb_t = Trk()
    pT = [al("pT%d" % i, [128, 512], BF16) for i in range(3)]
    pT_t = [Trk() for _ in pT]
    rd_off = (ar["off"] + 31) // 32 * 32
    rd = al("rd", [128, 4, 512], F32)
    rd_t = Trk()
    xin = [al("xin%d" % i, [128, D], F32, at=rd_off + 4096 * i) for i in range(2)]
    xin_t = [Trk() for _ in xin]
    attnT = al("attnT", [128, 4, 512], BF16)
    attnT_t = Trk()
    posi = al("posi", [16, 128], I32)
    posf = al("posf", [16, 128], F32)
    angt = al("angt", [128, 4, 16, 8], F32)
    angi = al("angi", [128, 16, 8], I32)

    region()
    wx = al("wx", [128, KC, 768], BF16)
    wx_t = Trk()
    wz = al("wz", [128, KC, 512], BF16)
    wz_t = Trk()
    wo_s = al("wo_s", [128, 4, D], BF16)
    wo_s_t = Trk()
    wdt = al("wdt", [128, KC, 16], BF16)
    wdt_t = Trk()
    dg = al("dg", [128, 6, 4, 128], BF16)
    dg_t = Trk()
    xraw = al("xraw", [128, 6, 3 + 512], BF16)
    xraw_t = [Trk() for _ in range(6)]
    bct = [al("bct%d" % i, [128, 2, 512], BF16) for i in range(2)]
    bct_t = [Trk() for _ in bct]
    zs = al("zs", [128, 4, 512], BF16)
    zs_t = [Trk() for _ in range(4)]
    x_f = al("x_f", [128, 4, 512], BF16)
    x_f_t = [Trk() for _ in range(4)]
    B_tm = al("B_tm", [128, 4, 128], BF16)
    B_tm_t = Trk()
    smA = al("smA", [128, 10, 256], F32)
    smA_t = Trk()
    ahl = al("ahl", [128, 2, 256], BF16)
    aTr = [al("aTr%d" % i, [128, 2, 8, 128], BF16) for i in range(2)]
    aTr_t = [Trk() for _ in range(2)]
    arg = al("arg", [128, 8, 128], F32)
    arg_t = Trk()
    Lt = [al("Lt%d" % i, [128, 8, 128], BF16) for i in range(2)]
    Lt_t = [Trk() for _ in range(2)]
    cbb = [al("cbb%d" % i, [128, 128], BF16) for i in range(2)]
    cbb_t = [Trk() for _ in range(2)]
    mT = [al("mT%d" % i, [128, 8, 128], BF16) for i in range(2)]
    mT_t = [Trk() for _ in range(2)]
    xgd = [al("xgd%d" % i, [128, 3, 512], BF16) for i in range(2)]
    xgd_t = [Trk() for _ in range(2)]
    hst = al("hst", [128, 512], F32)
    hst_t = Trk()
    hbf = al("hbf", [128, 512], BF16)
    hbf_t = Trk()
    ya = al("ya", [128, 512], F32)
    ya_t = Trk()
    yjunk = al("yjunk", [128, 512], BF16)
    ysc = al("ysc", [128, 4], F32)
    ysc_t = Trk()
    yn = al("yn", [128, 512], BF16)
    yn_t = Trk()
    yT = al("yT", [128, 4, 512], BF16)
    yT_t = Trk()
    cbrow = al("cbrow", [128, 640], BF16)
    cbrow_t = Trk()
    ssmw = al("ssmw", [128, 512], F32)
    ssmw_t = Trk()
    print("regionB bytes", ar["off"], "of", arena_size)

    region()
    ar["off"] = norm_end
    wg_off = (ar["off"] + 31) // 32 * 32
    wg = [al("wg%d" % i, [128, KC, 512], BF16) for i in range(2)]
    wu = [al("wu%d" % i, [128, KC, 512], BF16) for i in range(2)]
    wd = [al("wd%d" % i, [128, 4, D], BF16) for i in range(2)]
    wg_t = [Trk() for _ in range(2)]
    wu_t = [Trk() for _ in range(2)]
    wd_t = [Trk() for _ in range(2)]
    sg = [al("sg%d" % i, [128, 512], F32) for i in range(2)]
    sg_t = [Trk() for _ in sg]
    actT = al("actT", [128, 4, T], BF16)
    actT_t = [[Trk() for _ in range(4)] for _ in range(4)]
    fin = al("fin", [128, KC, 512], F32, at=wg_off)
    fin_t = Trk()
    osb = [al("osb%d" % i, [128, D], F32) for i in range(2)]
    osb_t = [Trk() for _ in osb]

    def cop(e, fn):
        S.op(e, fn, writes=[C])

    cop("dve", lambda: nc.vector.memset(ones_bf[:], 1.0))
    cop("dve", lambda: nc.vector.memset(zeros_bf[:], 0.0))
    cop("dve", lambda: nc.vector.memset(ones_f[:], 1.0))
    cop("dve", lambda: nc.vector.memset(e0_bf[:], 0.0))
    cop("dve", lambda: nc.vector.memset(e0_bf[0:1, :], 1.0))
    cop("dve", lambda: nc.vector.memset(epsc[:, 0:1], EPS))
    cop("dve", lambda: nc.vector.memset(epsc[:, 1:2], 1.0))
    for i in range(8):
        v = 500000.0 ** (-(2.0 * i) / 16.0) / TWO_PI
        cop("dve", lambda i=i, v=v: nc.vector.memset(invf[:, i:i + 1], v))

    def aff(out, in_, pattern, cm, base, op):
        S.op("pool", lambda: nc.gpsimd.affine_select(out=out, in_=in_, pattern=pattern, compare_op=op, fill=0.0,
                                                     base=base, channel_multiplier=cm), reads=[C], writes=[C])

    aff(tri_f[:], ones_f[:], [[1, 128]], -1, 0, ALU.is_ge)
    aff(ident_f[:], ones_f[:], [[1, 128]], -1, 0, ALU.is_equal)
    aff(mdiag[:], ones_bf[:], [[1, 128]], -1, 0, ALU.is_ge)
    aff(mprev[:], ones_bf[:], [[-1, 128]], 1, 0, ALU.is_ge)
    aff(ident_bf[:], ones_bf[:], [[1, 128]], -1, 0, ALU.is_equal)
    for g in range(4):
        aff(m3[:, g, :], ones_bf[:, 0:32], [[1, 32]], -1, 32 * g, ALU.is_ge)
    S.op("dve", lambda: nc.vector.tensor_tensor(out=NEGm[:], in0=mprev[:], in1=ident_bf[:], op=ALU.subtract), reads=[C], writes=[C])
    S.op("dve", lambda: nc.vector.tensor_scalar(out=NEGm[:], in0=NEGm[:], scalar1=-30000.0, scalar2=None, op0=ALU.mult),
         reads=[C], writes=[C])

    S.dma("sp", "d_c", prow[0:8, :], conv_w_d.rearrange("l k c -> (l k) c"), writes=[C])
    S.dma("sp", "d_c", prow[8:10, :], conv_b_d, writes=[C])
    S.dma("sp", "d_c", nrow[:, :], norms_d, writes=[C])
    S.dma("sp", "d_c", bc16[:, 0, :], dt_bias_d.rearrange("l n -> (l n)").partition_broadcast(128), writes=[C])
    S.dma("sp", "d_c", bc16[:, 1, :], a_log_d.rearrange("l n -> (l n)").partition_broadcast(128), writes=[C])
    S.dma("sp", "d_c", bc16[:, 2, :], d_skip_d.rearrange("l n -> (l n)").partition_broadcast(128), writes=[C])
    S.op("act", lambda: nc.scalar.activation(out=bc16[:, 1, :], in_=bc16[:, 1, :], func=AF.Exp), reads=[C], writes=[C])
    S.op("dve", lambda: nc.vector.tensor_scalar(out=bc16[:, 1, :], in0=bc16[:, 1, :], scalar1=-1.0, scalar2=None,
                                                op0=ALU.mult), reads=[C], writes=[C])
    for c in range(12):
        S.op("pe", lambda c=c: nc.tensor.transpose(out=PS[:, 0, c * 16:c * 16 + 10], in_=prow[0:10, c * 128:(c + 1) * 128],
                                                   identity=ident_f[0:10, 0:10]), reads=[C], writes=[PT[0]])
    S.op("dve", lambda: nc.vector.tensor_copy(out=pcol[:], in_=PS[:, 0, 0:192].rearrange("p (c k) -> p c k", k=16)[:, :, 0:10]),
         reads=[PT[0]], writes=[C])
    for c in range(8):
        S.op("pe", lambda c=c: nc.tensor.transpose(out=PS[:, 1, c * 8:c * 8 + 5], in_=nrow[0:5, c * 128:(c + 1) * 128],
                                                   identity=ident_f[0:5, 0:5]), reads=[C], writes=[PT[1]])
    S.op("dve", lambda: nc.vector.tensor_copy(out=ncol[:], in_=PS[:, 1, 0:64].rearrange("p (c k) -> p c k", k=8)[:, :, 0:5]),
         reads=[PT[1]], writes=[C])
    S.barrier()

    rr = {"ev": 0}

    def evac(out, in_, reads, writes, scale=None):
        rr["ev"] ^= 1
        if scale is not None or rr["ev"]:
            if scale is None:
                S.op("act", lambda: nc.scalar.activation(func=AF.Copy, out=out, in_=in_), reads=reads, writes=writes)
            else:
                S.op("act", lambda: nc.scalar.activation(func=AF.Identity, out=out, in_=in_, scale=scale), reads=reads, writes=writes)
        else:
            S.op("dve", lambda: nc.vector.tensor_copy(out=out, in_=in_), reads=reads, writes=writes)

    def blk(b):
        return slice(512 * b, 512 * (b + 1))

    def tl(t):
        return slice(128 * t, 128 * (t + 1))

    def rmsnorm(widx, final=False):
        for b in range(4):
            S.op("act", lambda b=b: nc.scalar.activation(out=sq[0][:], in_=hT[:, :, blk(b)], func=AF.Square),
                 reads=[hT_t[b]], writes=[sq_t[0]])
            bank = 4 + (b % 2)
            for c in range(KC):
                S.op("pe", lambda c=c, bank=bank: nc.tensor.matmul(PS[:, bank, :], lhsT=ones_bf[:], rhs=sq[0][:, c, :],
                                                                  start=(c == 0), stop=(c == KC - 1)),
                     reads=[sq_t[0], C], writes=[PT[bank]], inc=(c == KC - 1))
            S.op("act", lambda bank=bank: nc.scalar.activation(out=rstd[:], in_=PS[:, bank, :], func=AF.Ln,
                                                               scale=1.0 / D, bias=epsc[:, 0:1]),
                 reads=[PT[bank], C], writes=[rstd_t])
            S.op("act", lambda: nc.scalar.activation(out=rstd[:], in_=rstd[:], func=AF.Exp, scale=-0.5),
                 reads=[rstd_t], writes=[rstd_t])
            if not final:
                for c in range(KC):
                    S.op("dve", lambda c=c, b=b: nc.vector.scalar_tensor_tensor(
                        out=hnT[:, c, blk(b)], in0=hT[:, c, blk(b)], scalar=ncol[:, c, widx:widx + 1], in1=rstd[:],
                        op0=ALU.mult, op1=ALU.mult), reads=[hT_t[b], rstd_t, C], writes=[hnT_t[b]])
            else:
                yield b

    def load_w(dst, dst_t, src_rows_fn, nchunks, ds, q="pool"):
        for c in range(nchunks):
            S.dma(q, ds, dst[:, c, :], src_rows_fn(c), writes=[dst_t])

    def rope_tables(s):
        S.dma("sp", "d_pos", posi[:, :], pos_d[s].rearrange("(t p) -> t p", p=128), writes=[rope_t])
        S.op("dve", lambda: nc.vector.tensor_copy(out=posf[:], in_=posi[:]), reads=[rope_t], writes=[rope_t])
        S.op("pe", lambda: nc.tensor.transpose(out=PS[:, 5, 0:16], in_=posf[0:16, :], identity=ident_f[0:16, 0:16]),
             reads=[rope_t, C], writes=[PT[5]])
        S.op("dve", lambda: nc.vector.tensor_tensor(out=angt[:, 0], in0=PS[:, 5, 0:16].unsqueeze(2).broadcast_to([128, 16, 8]),
                                                    in1=invf[:].unsqueeze(1).broadcast_to([128, 16, 8]), op=ALU.mult),
             reads=[PT[5], C], writes=[rope_t])
        for which, shift, dst in ((0, 0.0, sinb), (1, 0.25, cosb)):
            f = angt[:, 1]
            if shift != 0.0:
                S.op("dve", lambda: nc.vector.tensor_scalar(out=angt[:, 1], in0=angt[:, 0], scalar1=shift, scalar2=None,
                                                            op0=ALU.add), reads=[rope_t], writes=[rope_t])
            else:
                S.op("dve", lambda: nc.vector.tensor_copy(out=angt[:, 1], in_=angt[:, 0]), reads=[rope_t], writes=[rope_t])
            S.op("dve", lambda: nc.vector.tensor_copy(out=angi[:], in_=f), reads=[rope_t], writes=[rope_t])
            S.op("dve", lambda: nc.vector.tensor_copy(out=angt[:, 2], in_=angi[:]), reads=[rope_t], writes=[rope_t])
            S.op("dve", lambda: nc.vector.tensor_tensor(out=angt[:, 1], in0=angt[:, 1], in1=angt[:, 2], op=ALU.subtract),
                 reads=[rope_t], writes=[rope_t])
            S.op("dve", lambda: nc.vector.tensor_single_scalar(out=angt[:, 2], in_=angt[:, 1], scalar=0.5, op=ALU.is_gt),
                 reads=[rope_t], writes=[rope_t])
            S.op("dve", lambda: nc.vector.tensor_tensor(out=angt[:, 1], in0=angt[:, 1], in1=angt[:, 2], op=ALU.subtract),
                 reads=[rope_t], writes=[rope_t])
            S.op("dve", lambda: nc.vector.tensor_single_scalar(out=angt[:, 2], in_=angt[:, 1], scalar=-0.5, op=ALU.is_lt),
                 reads=[rope_t], writes=[rope_t])
            S.op("dve", lambda: nc.vector.tensor_tensor(out=angt[:, 1], in0=angt[:, 1], in1=angt[:, 2], op=ALU.add),
                 reads=[rope_t], writes=[rope_t])
            S.op("act", lambda dst=dst: nc.scalar.activation(out=dst[:], in_=angt[:, 1], func=AF.Sin,
                                                             scale=TWO_PI * (1.0 - 2e-6)),
                 reads=[rope_t], writes=[rope_t])

    def load_x(s):
        for t in range(NT):
            sl = t % 2
            S.dma("sp", "d_xin%d" % sl, xin[sl][:, :], x_d[s, tl(t), :], writes=[xin_t[sl]])
            b0 = 2 * (t % 2)
            for c in range(KC):
                S.op("pe", lambda c=c, b0=b0, sl=sl: nc.tensor.transpose(
                    out=PS[:, b0 + c // 4, (c % 4) * 128:(c % 4 + 1) * 128], in_=xin[sl][:, c * 128:(c + 1) * 128],
                    identity=ident_f[:]), reads=[xin_t[sl], C], writes=[PT[b0 + c // 4]])
            for hh in range(2):
                evac(hT[:, 4 * hh:4 * hh + 4, tl(t)], PS[:, b0 + hh, :].rearrange("p (c t) -> p c t", t=128),
                     reads=[PT[b0 + hh]], writes=[hT_t[t // 4]])

    def attention(l):
        load_w(wq, wq_t, lambda c: w_in_d[l, c * 128:(c + 1) * 128, 0:768], KC, "d_wq")
        load_w(wo_a, wo_a_t, lambda c: w_out_d[l, c * 128:(c + 1) * 128, :], 4, "d_woa")
        for a in range(3):
            S.op("pool", lambda a=a: nc.gpsimd.memset(vaug[a][:], 1.0), writes=vaug_t[a])
        for t in range(NT):
            b0 = 2 * (t % 2)
            for c in range(KC):
                S.op("pe", lambda c=c, b0=b0, t=t: nc.tensor.matmul(PS[:, b0, :], lhsT=hnT[:, c, tl(t)], rhs=wq[:, c, 0:512],
                                                                    start=(c == 0), stop=(c == KC - 1)),
                     reads=[hnT_t[t // 4], wq_t], writes=[PT[b0]], inc=(c == KC - 1))
            for c in range(KC):
                S.op("pe", lambda c=c, b0=b0, t=t: nc.tensor.matmul(PS[:, b0 + 1, 0:256], lhsT=hnT[:, c, tl(t)],
                                                                    rhs=wq[:, c, 512:768], start=(c == 0), stop=(c == KC - 1)),
                     reads=[hnT_t[t // 4], wq_t], writes=[PT[b0 + 1]], inc=(c == KC - 1))
            S.op("act", lambda b0=b0: nc.scalar.activation(func=AF.Identity, out=qk_f[:, 0:512], in_=PS[:, b0, :], scale=0.125),
                 reads=[PT[b0]], writes=[qk_f_t])
            S.op("act", lambda b0=b0: nc.scalar.activation(func=AF.Copy, out=qk_f[:, 512:640], in_=PS[:, b0 + 1, 0:128]),
                 reads=[PT[b0 + 1]], writes=[qk_f_t])
            S.op("dve", lambda b0=b0, t=t: nc.vector.tensor_copy(out=vaug[0][:, t, 0, 0:64], in_=PS[:, b0 + 1, 128:192]),
                 reads=[PT[b0 + 1]], writes=[vaug_t[0][t]])
            S.op("dve", lambda b0=b0, t=t: nc.vector.tensor_copy(out=vaug[0][:, t, 1, 64:128], in_=PS[:, b0 + 1, 192:256]),
                 reads=[PT[b0 + 1]], writes=[vaug_t[0][t]])
            qv = qk_f[:].rearrange("p (h c) -> p h c", c=64)
            t1 = qv[:, :, 0:8]
            t2 = qv[:, :, 8:16]
            cs = cosb[:, t, :].unsqueeze(1).broadcast_to([128, 10, 8])
            sn = sinb[:, t, :].unsqueeze(1).broadcast_to([128, 10, 8])
            for i, (a_, b_) in enumerate(((t1, cs), (t2, sn), (t2, cs), (t1, sn))):
                S.op("dve", lambda i=i, a_=a_, b_=b_: nc.vector.tensor_tensor(out=rtmp[:, i], in0=a_, in1=b_, op=ALU.mult),
                     reads=[qk_f_t, rope_t], writes=[rtmp_t])
            S.op("dve", lambda: nc.vector.tensor_tensor(out=t1, in0=rtmp[:, 0], in1=rtmp[:, 1], op=ALU.subtract),
                 reads=[rtmp_t], writes=[qk_f_t])
            S.op("dve", lambda: nc.vector.tensor_tensor(out=t2, in0=rtmp[:, 2], in1=rtmp[:, 3], op=ALU.add),
                 reads=[rtmp_t], writes=[qk_f_t])
            S.op("dve", lambda: nc.vector.tensor_copy(out=qkb[:], in_=qk_f[:]), reads=[qk_f_t], writes=[qkb_t])
            pb = t % 2
            for j in range(5):
                S.op("pe", lambda j=j, pb=pb: nc.tensor.transpose(out=PB[:, pb, j * 128:(j + 1) * 128],
                                                                  in_=qkb[:, j * 128:(j + 1) * 128], identity=ident_bf[:]),
                     reads=[qkb_t, C], writes=[PBT[pb]])
            S.op("act", lambda pb=pb, t=t: nc.scalar.activation(func=AF.Copy, out=qT[:, :, tl(t)],
                                                          in_=PB[:, pb, 0:512].rearrange("p (j t) -> p j t", t=128)),
                 reads=[PBT[pb]], writes=[qT_t[t]])
            S.op("dve", lambda pb=pb, t=t: nc.vector.tensor_copy(out=kT[:, tl(t)], in_=PB[:, pb, 512:640]),
                 reads=[PBT[pb]], writes=[kT_t[t]])
        for a, dil in ((1, 4), (2, 16)):
            for grp in range(4):
                bank = 4 + (grp % 2)
                for i in range(4):
                    bi = grp * 4 + i
                    if dil == 4:
                        n, r = bi // 4, bi % 4
                        tsl = slice(512 * n + r, 512 * (n + 1), 4)
                        hts = [hnT_t[n]]
                    else:
                        tsl = slice(bi, T, 16)
                        hts = hnT_t
                    for c in range(KC):
                        S.op("pe", lambda c=c, i=i, bank=bank, tsl=tsl: nc.tensor.matmul(
                            PS[:, bank, i * 128:(i + 1) * 128], lhsT=hnT[:, c, tsl], rhs=wq[:, c, 640:768],
                            start=(c == 0), stop=(c == KC - 1)), reads=hts + [wq_t], writes=[PT[bank]], inc=(c == KC - 1))
                pv = PS[:, bank, :].rearrange("p (i c) -> p i c", c=128)
                wts = [vaug_t[a][grp * 4 + i] for i in range(4)]
                S.op("act", lambda a=a, grp=grp, pv=pv: nc.scalar.activation(func=AF.Copy, out=vaug[a][:, grp * 4:grp * 4 + 4, 0, 0:64],
                                                                       in_=pv[:, :, 0:64]), reads=[PT[bank]], writes=wts)
                S.op("dve", lambda a=a, grp=grp, pv=pv: nc.vector.tensor_copy(out=vaug[a][:, grp * 4:grp * 4 + 4, 1, 64:128],
                                                                              in_=pv[:, :, 64:128]), reads=[PT[bank]], writes=wts)
        state = {"s": 0, "p": 0}

        def score_block(kv, lhs_sl, q_sl_list, mask_ap, k_trks, q_trks):
            rows = slice(64 * kv, 64 * kv + 64)
            sbank = 4 + state["s"]
            state["s"] ^= 1
            pi = state["p"]
            state["p"] = (pi + 1) % 3
            n = len(q_sl_list)
            for i, (off, ncol, qsl) in enumerate(q_sl_list):
                S.op("pe", lambda off=off, ncol=ncol, qsl=qsl, lsl=lhs_sl[i]: nc.tensor.matmul(
                    PS[:, sbank, off:off + 4 * ncol].rearrange("p (g q) -> p g q", q=ncol),
                    lhsT=kT[rows, lsl], rhs=qT[rows, :, qsl], start=True, stop=True),
                    reads=k_trks + q_trks, writes=[PT[sbank]], inc=(i == n - 1))
            S.op("act", lambda: nc.scalar.activation(out=pT[pi][:], in_=PS[:, sbank, :], func=AF.Exp),
                 reads=[PT[sbank]], writes=[pT_t[pi]])
            S.op("dve", lambda: nc.vector.tensor_tensor(out=pT[pi][:].rearrange("p (a q) -> p a q", q=mask_ap.shape[-1]),
                                                        in0=pT[pi][:].rearrange("p (a q) -> p a q", q=mask_ap.shape[-1]),
                                                        in1=mask_ap, op=ALU.mult),
                 reads=[pT_t[pi], C], writes=[pT_t[pi]])
            return pi

        OT = [PT[n] for n in range(4)]

        def zero_init():
            for n in range(4):
                S.op("pe", lambda n=n: nc.tensor.matmul(PS[:, n, :], lhsT=zeros_bf[:], rhs=hnT[:, 0, 0:512],
                                                        start=True, stop=False), reads=[hnT_t[0], C], writes=[PT[n]])

        def normalize(kv):
            nrows = slice(64 * kv, 64 * kv + 64)
            drows = slice(64 * (1 - kv), 64 * (1 - kv) + 64)
            Ov = PS[:, 0:4, :]
            S.op("act", lambda: nc.scalar.activation(out=rd[nrows], in_=Ov[drows], func=AF.Ln), reads=OT, writes=[rd_t])
            S.op("act", lambda: nc.scalar.activation(out=rd[nrows], in_=rd[nrows], func=AF.Exp, scale=-1.0),
                 reads=[rd_t], writes=[rd_t])
            S.op("dve", lambda: nc.vector.tensor_tensor(
                out=attnT[nrows].rearrange("p g (n t) -> p n g t", t=128),
                in0=Ov[nrows].rearrange("p n (g t) -> p n g t", t=128),
                in1=rd[nrows].rearrange("p n (g t) -> p n g t", t=128), op=ALU.mult),
                reads=OT + [rd_t], writes=[attnT_t])

        def out_proj(G):
            for dc in range(KC):
                bank = 4 + (dc % 2)
                for j in range(4):
                    S.op("pe", lambda j=j, dc=dc, bank=bank: nc.tensor.matmul(
                        PS[:, bank, :], lhsT=wo_a[:, j, dc * 128:(dc + 1) * 128], rhs=attnT[:, j, :],
                        start=(j == 0), stop=(j == 3)), reads=[wo_a_t, attnT_t], writes=[PT[bank]], inc=(j == 3))
                S.op("dve", lambda dc=dc, bank=bank, G=G: nc.vector.tensor_tensor(
                    out=hT[:, dc, blk(G)], in0=hT[:, dc, blk(G)], in1=PS[:, bank, :], op=ALU.add),
                    reads=[PT[bank], hT_t[G]], writes=[hT_t[G]])

        blocks = []
        for G in range(4):
            for kv in range(2):
                first = [True]
                grp = []
                for n in range(4):
                    qb = 4 * G + n
                    for kb in (qb - 1, qb):
                        if kb < 0:
                            continue
                        mask = (mdiag if kb == qb else mprev)[:].unsqueeze(1).broadcast_to([128, 4, 128])

                        def sc(kv=kv, kb=kb, qb=qb, mask=mask):
                            return score_block(kv, [tl(kb)], [(0, 128, tl(qb))], mask, [kT_t[kb]], [qT_t[qb]])

                        def pv(pi, kv=kv, kb=kb, n=n):
                            S.op("pe", lambda: nc.tensor.matmul(
                                PS[:, n, :], lhsT=vaug[0][:, kb, kv, :], rhs=pT[pi][:], start=False, stop=False,
                                skip_group_check=True), reads=[pT_t[pi], vaug_t[0][kb]], writes=[PT[n]])
                        grp.append((sc, pv))
                for r in range(4):
                    qsl = slice(512 * G + r, 512 * (G + 1), 4)
                    for Gk in (G - 1, G):
                        if Gk < 0:
                            continue
                        ksl = slice(512 * Gk + r, 512 * (Gk + 1), 4)
                        mask = (mdiag if Gk == G else mprev)[:].unsqueeze(1).broadcast_to([128, 4, 128])

                        def sc(kv=kv, ksl=ksl, qsl=qsl, mask=mask, Gk=Gk, G=G):
                            return score_block(kv, [ksl], [(0, 128, qsl)], mask, kT_t[4 * Gk:4 * Gk + 4], qT_t[4 * G:4 * G + 4])

                        def pv(pi, kv=kv, Gk=Gk, r=r):
                            pvv = pT[pi][:].rearrange("p (g q) -> p g q", q=128)
                            for n in range(4):
                                S.op("pe", lambda n=n: nc.tensor.matmul(
                                    PS[:, n, :].rearrange("p (g t) -> p g t", t=128)[:, :, r:128:4],
                                    lhsT=vaug[1][:, Gk * 4 + r, kv, :], rhs=pvv[:, :, 32 * n:32 * n + 32], start=False,
                                    stop=False, skip_group_check=True), reads=[pT_t[pi], vaug_t[1][Gk * 4 + r]],
                                    writes=[PT[n]], inc=(n == 3))
                        grp.append((sc, pv))
                for r4 in range(4):
                    lhs = [slice(4 * r4 + i, T, 16) for i in range(4)]
                    qsl = [(i * 128, 32, slice(512 * G + 4 * r4 + i, 512 * (G + 1), 16)) for i in range(4)]
                    mask = m3[:, G, :].unsqueeze(1).broadcast_to([128, 16, 32])

                    def sc(kv=kv, lhs=lhs, qsl=qsl, mask=mask, G=G):
                        return score_block(kv, lhs, qsl, mask, kT_t, qT_t[4 * G:4 * G + 4])

                    def pv(pi, kv=kv, r4=r4):
                        pvv = pT[pi][:].rearrange("p (i g q) -> p i g q", g=4, q=32)
                        for i in range(4):
                            r = 4 * r4 + i
                            for n in range(4):
                                last = (r == 15)
                                S.op("pe", lambda n=n, i=i, r=r, last=last: nc.tensor.matmul(
                                    PS[:, n, :].rearrange("p (g t) -> p g t", t=128)[:, :, r:128:16],
                                    lhsT=vaug[2][:, r, kv, :], rhs=pvv[:, i, :, 8 * n:8 * n + 8], start=False, stop=last,
                                    skip_group_check=True), reads=[pT_t[pi], vaug_t[2][r]], writes=[PT[n]],
                                    inc=(i == 3 and n == 3))
                    grp.append((sc, pv))
                for bi, (sc, pv) in enumerate(grp):
                    blocks.append((sc, pv, bi == 0, bi == len(grp) - 1, G, kv))
        pend = None

        def flush(p):
            sc, pv, isfirst, islast, G, kv, pi = p
            if isfirst:
                zero_init()
            pv(pi)
            if islast:
                for n in range(4):
                    S.op("pe", lambda n=n: nc.tensor.matmul(PS[:, n, :], lhsT=zeros_bf[:], rhs=hnT[:, 0, 0:512],
                                                            start=False, stop=True), reads=[hnT_t[0], C], writes=[PT[n]],
                         inc=(n == 3))
                normalize(kv)
                if kv == 1:
                    out_proj(G)

        for (sc, pv, isfirst, islast, G, kv) in blocks:
            pi = sc()
            if pend is not None:
                flush(pend)
            pend = (sc, pv, isfirst, islast, G, kv, pi)
        flush(pend)

    U_, T1_, DTV_, W2_, DSK_, A_, NACS_, EA_, EAT_, DEC_ = range(10)

    def ssd_prep(l):
        sa = lambda i: smA[:, i, :]
        v3 = lambda ap: ap.rearrange("p (t h) -> p t h", h=16)
        bc = lambda j: bc16[:, j, l * 16:l * 16 + 16].unsqueeze(1).broadcast_to([128, 16, 16])
        load_w(wdt, wdt_t, lambda c: w_in_d[l, c * 128:(c + 1) * 128, 3328:3344], KC, "d_wdt")
        for t in range(NT):
            for c in range(KC):
                S.op("pe", lambda c=c, t=t: nc.tensor.matmul(PS[:, 2, t * 16:(t + 1) * 16], lhsT=hnT[:, c, tl(t)], rhs=wdt[:, c, :],
                                                             start=(c == 0), stop=(c == KC - 1)),
                     reads=[hnT_t[t // 4], wdt_t], writes=[PT[2]], inc=(c == KC - 1))
        S.op("dve", lambda: nc.vector.tensor_tensor(out=v3(sa(U_)), in0=PS[:, 2, 0:256].rearrange("p (t h) -> p t h", h=16),
                                                    in1=bc(0), op=ALU.add),
             reads=[PT[2], C], writes=[smA_t])
        S.op("act", lambda: nc.scalar.activation(out=sa(T1_), in_=sa(U_), func=AF.Abs), reads=[smA_t], writes=[smA_t])
        S.op("act", lambda: nc.scalar.activation(out=sa(T1_), in_=sa(T1_), func=AF.Exp, scale=-1.0),
             reads=[smA_t], writes=[smA_t])
        S.op("act", lambda: nc.scalar.activation(out=sa(T1_), in_=sa(T1_), func=AF.Ln, bias=epsc[:, 1:2]),
             reads=[smA_t, C], writes=[smA_t])
        S.op("dve", lambda: nc.vector.tensor_single_scalar(out=sa(U_), in_=sa(U_), scalar=0.0, op=ALU.max),
             reads=[smA_t], writes=[smA_t])
        S.op("dve", lambda: nc.vector.tensor_tensor(out=sa(DTV_), in0=sa(U_), in1=sa(T1_), op=ALU.add),
             reads=[smA_t], writes=[smA_t])
        S.op("dve", lambda: nc.vector.tensor_tensor(out=v3(sa(A_)), in0=v3(sa(DTV_)), in1=bc(1), op=ALU.mult),
             reads=[smA_t, C], writes=[smA_t])
        S.op("dve", lambda: nc.vector.tensor_copy(out=v3(sa(DSK_)), in_=bc(2)), reads=[C], writes=[smA_t])
        S.op("pe", lambda: nc.tensor.matmul(PS[:, 0, 0:256], lhsT=tri_f[:], rhs=sa(A_), start=True, stop=True),
             reads=[smA_t, C], writes=[PT[0]])
        S.op("pe", lambda: nc.tensor.matmul(PS[:, 1, 0:256], lhsT=ones_f[:], rhs=sa(A_), start=True, stop=True),
             reads=[smA_t, C], writes=[PT[1]])
        S.op("dve", lambda: nc.vector.tensor_scalar(out=sa(NACS_), in0=PS[:, 0, 0:256], scalar1=-1.0, scalar2=None,
                                                    op0=ALU.mult), reads=[PT[0]], writes=[smA_t])
        S.op("act", lambda: nc.scalar.activation(out=sa(EA_), in_=PS[:, 0, 0:256], func=AF.Exp), reads=[PT[0]], writes=[smA_t])
        S.op("act", lambda: nc.scalar.activation(out=sa(EAT_), in_=PS[:, 1, 0:256], func=AF.Exp), reads=[PT[1]], writes=[smA_t])
        S.op("dve", lambda: nc.vector.tensor_tensor(out=sa(DEC_), in0=PS[:, 1, 0:256], in1=sa(NACS_), op=ALU.add),
             reads=[PT[1], smA_t], writes=[smA_t])
        S.op("act", lambda: nc.scalar.activation(out=sa(DEC_), in_=sa(DEC_), func=AF.Exp), reads=[smA_t], writes=[smA_t])
        S.op("dve", lambda: nc.vector.tensor_tensor(out=sa(W2_), in0=sa(DTV_), in1=sa(DEC_), op=ALU.mult),
             reads=[smA_t], writes=[smA_t])
        S.op("dve", lambda: nc.vector.tensor_copy(out=ahl[:, 0, :], in_=sa(A_)), reads=[smA_t], writes=[smA_t])
        S.op("dve", lambda: nc.vector.tensor_tensor(out=ahl[:, 1, :], in0=sa(A_), in1=ahl[:, 0, :], op=ALU.subtract),
             reads=[smA_t], writes=[smA_t])

    def ssd(l, gp):
        xo = 1792 + 512 * gp
        bo = 2816 + 128 * gp
        co = 3072 + 128 * gp
        for c in range(KC):
            rs = slice(c * 128, (c + 1) * 128)
            S.dma("pool", "d_wx", wx[:, c, 0:512], w_in_d[l, rs, xo:xo + 512], writes=[wx_t])
            S.dma("pool", "d_wx", wx[:, c, 512:640], w_in_d[l, rs, bo:bo + 128], writes=[wx_t])
            S.dma("pool", "d_wx", wx[:, c, 640:768], w_in_d[l, rs, co:co + 128], writes=[wx_t])
        load_w(wz, wz_t, lambda c: w_in_d[l, c * 128:(c + 1) * 128, 768 + 512 * gp:768 + 512 * gp + 512], KC, "d_wz")
        load_w(wo_s, wo_s_t, lambda c: w_out_d[l, 512 + 512 * gp + c * 128:512 + 512 * gp + (c + 1) * 128, :], 4, "d_wos")
        S.op("dve", lambda: nc.vector.memset(cbrow[:], 0.0), writes=[cbrow_t])
        S.dma("pool", "d_cb", cbrow[0:1, 0:512], conv_b_d[l:l + 1, 512 * gp:512 * gp + 512], writes=[cbrow_t])
        S.dma("pool", "d_cb", cbrow[0:1, 512:640], conv_b_d[l:l + 1, 1024 + 128 * gp:1024 + 128 * gp + 128], writes=[cbrow_t])
        S.dma("sp", "d_sw", ssmw[:, :], ssm_norm_d[l, 512 * gp:512 * gp + 512].partition_broadcast(128), writes=[ssmw_t])
        pcs = [gp * 4 + i for i in range(4)] + [8 + gp, 10 + gp]
        for ch in range(6):
            for k in range(4):
                S.op("dve", lambda ch=ch, k=k: nc.vector.tensor_scalar(
                    out=dg[:, ch, k, :], in0=ident_f[:], scalar1=pcol[:, pcs[ch], l * 4 + k:l * 4 + k + 1], scalar2=None,
                    op0=ALU.mult), reads=[C], writes=[dg_t])
        S.op("dve", lambda: nc.vector.memset(hst[:], 0.0), writes=[hst_t])
        S.op("dve", lambda: nc.vector.memset(hbf[:], 0.0), writes=[hbf_t])
        dsk = bc16[:, 2, l * 16 + 8 * gp:l * 16 + 8 * gp + 8]
        b8 = lambda ap: ap.unsqueeze(2).broadcast_to([128, 8, 64])
        v8 = lambda ap: ap.rearrange("p (e c) -> p e c", c=64)

        def smc(i, t):
            return smA[:, i, t * 16 + 8 * gp:t * 16 + 8 * gp + 8]

        for G in range(4):
            bc = bct[G % 2]
            bc_t = bct_t[G % 2]
            if G == 0:
                S.op("dve", lambda: nc.vector.memset(xraw[:, :, 0:3], 0.0), writes=xraw_t)
            else:
                S.op("dve", lambda: nc.vector.tensor_copy(out=xraw[:, :, 0:3], in_=xraw[:, :, 512:515]),
                     reads=xraw_t, writes=xraw_t)
            for ch in range(6):
                bank = ch % 2
                for c in range(KC):
                    S.op("pe", lambda c=c, ch=ch, bank=bank: nc.tensor.matmul(
                        PS[:, bank, :], lhsT=wx[:, c, ch * 128:(ch + 1) * 128], rhs=hnT[:, c, blk(G)],
                        start=(c == 0), stop=(c == KC - 1)), reads=[wx_t, hnT_t[G]], writes=[PT[bank]], inc=(c == KC - 1))
                evac(xraw[:, ch, 3:515], PS[:, bank, :], reads=[PT[bank]], writes=[xraw_t[ch]])
            for idx, ch in ((0, 4), (1, 5)):
                for k in range(4):
                    S.op("pe", lambda k=k, ch=ch: nc.tensor.matmul(
                        PS[:, 2, :], lhsT=dg[:, ch, k, :], rhs=xraw[:, ch, k:k + 512],
                        start=(k == 0), stop=(k == 3)), reads=[dg_t, xraw_t[ch]], writes=[PT[2]], inc=(k == 3))
                S.op("act", lambda idx=idx, ch=ch: nc.scalar.activation(
                    out=bc[:, idx, :], in_=PS[:, 2, :], func=AF.Silu, bias=pcol[:, pcs[ch], 8 + l:9 + l]),
                    reads=[PT[2], C], writes=[bc_t])
            S.op("pe", lambda: nc.tensor.matmul(PS[:, 5, :], lhsT=zeros_bf[:], rhs=hnT[:, 0, 0:512],
                                                start=True, stop=False), reads=[C, hnT_t[0]], writes=[PT[5]], inc=False)
            for tt in range(4):
                for k in range(4):
                    S.op("pe", lambda tt=tt, k=k: nc.tensor.matmul(
                        PS[:, 5, tt * 128:(tt + 1) * 128], lhsT=xraw[:, 4, 128 * tt + k:128 * tt + k + 128], rhs=dg[:, 4, k, :],
                        start=False, stop=False, skip_group_check=True), reads=[dg_t, xraw_t[4]], writes=[PT[5]],
                        inc=False)
            S.op("pe", lambda: nc.tensor.matmul(PS[:, 5, :].rearrange("p (a c) -> p a c", c=128), lhsT=e0_bf[:],
                                                rhs=cbrow[:, 512:640].unsqueeze(1).broadcast_to([128, 4, 128]),
                                                start=False, stop=True), reads=[C, cbrow_t], writes=[PT[5]])
            S.op("act", lambda: nc.scalar.activation(out=B_tm[:].rearrange("p a c -> p (a c)"), in_=PS[:, 5, :], func=AF.Silu),
                 reads=[PT[5]], writes=[B_tm_t])
            for tt in range(4):
                t = 4 * G + tt
                xb = 3 + (tt % 2)
                S.op("pe", lambda xb=xb: nc.tensor.matmul(PS[:, xb, :], lhsT=zeros_bf[:], rhs=hnT[:, 0, 0:512],
                                                          start=True, stop=False), reads=[C, hnT_t[0]], writes=[PT[xb]], inc=False)
                for i in range(4):
                    for k in range(4):
                        S.op("pe", lambda i=i, k=k, tt=tt, xb=xb: nc.tensor.matmul(
                            PS[:, xb, i * 128:(i + 1) * 128], lhsT=xraw[:, i, 128 * tt + k:128 * tt + k + 128],
                            rhs=dg[:, i, k, :], start=False, stop=False, skip_group_check=True),
                            reads=[dg_t, xraw_t[i]], writes=[PT[xb]], inc=False)
                S.op("pe", lambda xb=xb: nc.tensor.matmul(PS[:, xb, :], lhsT=e0_bf[:], rhs=cbrow[:, 0:512],
                                                          start=False, stop=True), reads=[C, cbrow_t], writes=[PT[xb]])
                S.op("act", lambda tt=tt, xb=xb: nc.scalar.activation(out=x_f[:, tt, :], in_=PS[:, xb, :], func=AF.Silu),
                     reads=[PT[xb]], writes=[x_f_t[tt]])
                zb = tt % 2
                for c in range(KC):
                    S.op("pe", lambda c=c, t=t, zb=zb: nc.tensor.matmul(PS[:, zb, :], lhsT=hnT[:, c, tl(t)], rhs=wz[:, c, :],
                                                                        start=(c == 0), stop=(c == KC - 1)),
                         reads=[hnT_t[G], wz_t], writes=[PT[zb]], inc=(c == KC - 1))
                S.op("act", lambda tt=tt, zb=zb: nc.scalar.activation(out=zs[:, tt, :], in_=PS[:, zb, :], func=AF.Silu),
                     reads=[PT[zb]], writes=[zs_t[tt]])

            def stageA(tt):
                t = 4 * G + tt
                s2 = tt % 2
                cs = slice(t * 16 + 8 * gp, t * 16 + 8 * gp + 8)
                for hb in range(2):
                    S.op("pe", lambda hb=hb: nc.tensor.matmul(PS[:, hb, :], lhsT=ident_bf[:],
                                                              rhs=NEGm[:].unsqueeze(1).broadcast_to([128, 4, 128]),
                                                              start=True, stop=False), reads=[C], writes=[PT[hb]], inc=False)
                    for e4 in range(4):
                        col = t * 16 + 8 * gp + 4 * hb + e4
                        for hl in range(2):
                            lastm = (e4 == 3 and hl == 1)
                            S.op("pe", lambda hb=hb, e4=e4, hl=hl, col=col, lastm=lastm: nc.tensor.matmul(
                                PS[:, hb, e4 * 128:(e4 + 1) * 128], lhsT=ahl[:, hl, col:col + 1].broadcast_to([128, 128]),
                                rhs=mdiag[:], start=False, stop=lastm), reads=[smA_t, C], writes=[PT[hb]], inc=lastm)
                S.op("dve", lambda: nc.vector.tensor_tensor(
                    out=arg[:], in0=PS[:, 0:2, :].rearrange("p b (e l) -> p (b e) l", l=128),
                    in1=smc(NACS_, t).unsqueeze(2).broadcast_to([128, 8, 128]), op=ALU.add),
                    reads=[PT[0], PT[1], smA_t], writes=[arg_t])
                S.op("act", lambda: nc.scalar.activation(out=Lt[s2][:], in_=arg[:], func=AF.Exp), reads=[arg_t], writes=[Lt_t[s2]])
                S.op("pe", lambda: nc.tensor.matmul(PS[:, 2, s2 * 128:(s2 + 1) * 128], lhsT=bc[:, 0, tl(tt)], rhs=bc[:, 1, tl(tt)],
                                                    start=True, stop=True), reads=[bc_t], writes=[PT[2]])
                S.op("act", lambda: nc.scalar.activation(func=AF.Copy, out=cbb[s2][:], in_=PS[:, 2, s2 * 128:(s2 + 1) * 128]),
                     reads=[PT[2]], writes=[cbb_t[s2]])
                S.op("dve", lambda: nc.vector.tensor_tensor(out=mT[s2][:], in0=Lt[s2][:],
                                                            in1=cbb[s2][:].unsqueeze(1).broadcast_to([128, 8, 128]), op=ALU.mult),
                     reads=[Lt_t[s2], cbb_t[s2]], writes=[mT_t[s2]])
                S.op("pool", lambda: nc.gpsimd.tensor_tensor(
                    out=xgd[s2][:].rearrange("p j (e c) -> p j e c", c=64),
                    in0=v8(x_f[:, tt, :]).unsqueeze(1).broadcast_to([128, 3, 8, 64]),
                    in1=smA[:, DTV_:DTV_ + 3, cs].unsqueeze(3).broadcast_to([128, 3, 8, 64]), op=ALU.mult),
                    reads=[x_f_t[tt], smA_t, C], writes=[xgd_t[s2]])

            def stageB(tt):
                t = 4 * G + tt
                s2 = tt % 2
                S.op("pe", lambda: nc.tensor.matmul(PS[:, 3, :], lhsT=zeros_bf[:], rhs=hnT[:, 0, 0:512], start=True, stop=False),
                     reads=[hnT_t[0], C], writes=[PT[3]], inc=False)
                for e in range(8):
                    S.op("pe", lambda e=e: nc.tensor.matmul(PS[:, 3, e * 64:(e + 1) * 64], lhsT=mT[s2][:, e, :],
                                                            rhs=xgd[s2][:, 0, e * 64:(e + 1) * 64], start=False, stop=False,
                                                            skip_group_check=True),
                         reads=[mT_t[s2], xgd_t[s2]], writes=[PT[3]], inc=False)
                S.op("pe", lambda: nc.tensor.matmul(PS[:, 3, :], lhsT=ident_bf[:], rhs=xgd[s2][:, 2, :], start=False, stop=True),
                     reads=[xgd_t[s2], C], writes=[PT[3]])
                S.op("pe", lambda: nc.tensor.matmul(PS[:, 4, :], lhsT=bc[:, 1, tl(tt)], rhs=hbf[:], start=True, stop=True),
                     reads=[bc_t, hbf_t], writes=[PT[4]])
                S.op("pe", lambda: nc.tensor.matmul(PS[:, 5, :], lhsT=B_tm[:, tt, :], rhs=xgd[s2][:, 1, :], start=True, stop=True),
                     reads=[B_tm_t, xgd_t[s2]], writes=[PT[5]])
                S.op("pool", lambda: nc.gpsimd.tensor_tensor(out=v8(hst[:]), in0=v8(hst[:]), in1=b8(smc(EAT_, t)), op=ALU.mult),
                     reads=[hst_t, smA_t], writes=[hst_t])
                S.op("dve", lambda: nc.vector.tensor_tensor(out=hst[:], in0=hst[:], in1=PS[:, 5, :], op=ALU.add),
                     reads=[hst_t, PT[5]], writes=[hst_t])
                S.op("act", lambda: nc.scalar.activation(func=AF.Copy, out=hbf[:], in_=hst[:]), reads=[hst_t], writes=[hbf_t])
                S.op("dve", lambda: nc.vector.tensor_tensor(out=v8(ya[:]), in0=v8(PS[:, 4, :]), in1=b8(smc(EA_, t)), op=ALU.mult),
                     reads=[PT[4], smA_t], writes=[ya_t])
                S.op("dve", lambda: nc.vector.tensor_tensor(out=ya[:], in0=ya[:], in1=PS[:, 3, :], op=ALU.add),
                     reads=[PT[3], ya_t], writes=[ya_t])
                S.op("dve", lambda: nc.vector.tensor_tensor(out=ya[:], in0=ya[:], in1=zs[:, tt, :], op=ALU.mult),
                     reads=[ya_t, zs_t[tt]], writes=[ya_t])
                S.op("act", lambda: nc.scalar.activation(out=yjunk[:], in_=ya[:], func=AF.Square, accum_out=ysc[:, 0:1]),
                     reads=[ya_t], writes=[ysc_t])
                S.op("act", lambda: nc.scalar.activation(out=ysc[:, 1:2], in_=ysc[:, 0:1], func=AF.Ln, scale=1.0 / 512,
                                                         bias=epsc[:, 0:1]), reads=[ysc_t, C], writes=[ysc_t])
                S.op("act", lambda: nc.scalar.activation(out=ysc[:, 2:3], in_=ysc[:, 1:2], func=AF.Exp, scale=-0.5),
                     reads=[ysc_t], writes=[ysc_t])
                S.op("dve", lambda: nc.vector.scalar_tensor_tensor(out=yn[:], in0=ya[:], scalar=ysc[:, 2:3], in1=ssmw[:],
                                                                   op0=ALU.mult, op1=ALU.mult),
                     reads=[ya_t, ysc_t, ssmw_t], writes=[yn_t])
                pb = tt % 2
                for j in range(4):
                    S.op("pe", lambda j=j, pb=pb: nc.tensor.transpose(out=PB[:, pb, j * 128:(j + 1) * 128],
                                                                      in_=yn[:, j * 128:(j + 1) * 128], identity=ident_bf[:]),
                         reads=[yn_t, C], writes=[PBT[pb]], inc=(j == 3))
                S.op("act", lambda pb=pb, tt=tt: nc.scalar.activation(func=AF.Copy, out=yT[:, :, tl(tt)],
                                                                      in_=PB[:, pb, 0:512].rearrange("p (j t) -> p j t", t=128)),
                     reads=[PBT[pb]], writes=[yT_t])

            stageA(0)
            for tt in range(4):
                if tt + 1 < 4:
                    stageA(tt + 1)
                stageB(tt)
            for dc in range(KC):
                bank = dc % 2
                for j in range(4):
                    S.op("pe", lambda j=j, dc=dc, bank=bank: nc.tensor.matmul(
                        PS[:, bank, :], lhsT=wo_s[:, j, dc * 128:(dc + 1) * 128], rhs=yT[:, j, :],
                        start=(j == 0), stop=(j == 3)), reads=[wo_s_t, yT_t], writes=[PT[bank]], inc=(j == 3))
                S.op("dve", lambda dc=dc, bank=bank, G=G: nc.vector.tensor_tensor(
                    out=hT[:, dc, blk(G)], in0=hT[:, dc, blk(G)], in1=PS[:, bank, :], op=ALU.add),
                    reads=[PT[bank], hT_t[G]], writes=[hT_t[G]])

    def ffn(l):
        ngrp = 6
        for fg in range(ngrp):
            nch = 4 if fg < 5 else 2
            f0 = fg * 512
            sl = fg % 2
            for c in range(KC):
                rs = slice(c * 128, (c + 1) * 128)
                S.dma("pool", "d_wg%d" % sl, wg[sl][:, c, 0:128 * nch], w_gate_d[l, rs, f0:f0 + 128 * nch], writes=[wg_t[sl]])
                S.dma("pool", "d_wu%d" % sl, wu[sl][:, c, 0:128 * nch], w_up_d[l, rs, f0:f0 + 128 * nch], writes=[wu_t[sl]])
            for j in range(nch):
                S.dma("pool", "d_wd%d" % sl, wd[sl][:, j, :], w_down_d[l, f0 + j * 128:f0 + (j + 1) * 128, :], writes=[wd_t[sl]])
            for j in range(nch):
                for b in range(4):
                    gb = 2 * (b % 2)
                    for c in range(KC):
                        S.op("pe", lambda c=c, j=j, b=b, gb=gb: nc.tensor.matmul(
                            PS[:, gb, :], lhsT=wg[sl][:, c, j * 128:(j + 1) * 128], rhs=hnT[:, c, blk(b)],
                            start=(c == 0), stop=(c == KC - 1)), reads=[wg_t[sl], hnT_t[b]], writes=[PT[gb]], inc=(c == KC - 1))
                    for c in range(KC):
                        S.op("pe", lambda c=c, j=j, b=b, gb=gb: nc.tensor.matmul(
                            PS[:, gb + 1, :], lhsT=wu[sl][:, c, j * 128:(j + 1) * 128], rhs=hnT[:, c, blk(b)],
                            start=(c == 0), stop=(c == KC - 1)), reads=[wu_t[sl], hnT_t[b]], writes=[PT[gb + 1]],
                            inc=(c == KC - 1))
                    si = b % 2
                    S.op("act", lambda gb=gb, si=si: nc.scalar.activation(out=sg[si][:], in_=PS[:, gb, :], func=AF.Silu),
                         reads=[PT[gb]], writes=[sg_t[si]])
                    S.op("dve", lambda gb=gb, si=si, j=j, b=b: nc.vector.tensor_tensor(
                        out=actT[:, j, blk(b)], in0=sg[si][:], in1=PS[:, gb + 1, :], op=ALU.mult),
                        reads=[sg_t[si], PT[gb + 1]], writes=[actT_t[j][b]])
            for dc in range(KC):
                for b in range(4):
                    bank = 4 + ((dc * 4 + b) % 2)
                    for j in range(nch):
                        S.op("pe", lambda j=j, dc=dc, b=b, bank=bank: nc.tensor.matmul(
                            PS[:, bank, :], lhsT=wd[sl][:, j, dc * 128:(dc + 1) * 128], rhs=actT[:, j, blk(b)],
                            start=(j == 0), stop=(j == nch - 1)), reads=[wd_t[sl], actT_t[j][b]], writes=[PT[bank]],
                            inc=(j == nch - 1))
                    S.op("dve", lambda dc=dc, b=b, bank=bank: nc.vector.tensor_tensor(
                        out=hT[:, dc, blk(b)], in0=hT[:, dc, blk(b)], in1=PS[:, bank, :], op=ALU.add),
                        reads=[PT[bank], hT_t[b]], writes=[hT_t[b]])

    def final_out(s):
        for b in rmsnorm(4, final=True):
            for c in range(KC):
                S.op("dve", lambda c=c, b=b: nc.vector.scalar_tensor_tensor(
                    out=fin[:, c, :], in0=hT[:, c, blk(b)], scalar=ncol[:, c, 4:5], in1=rstd[:],
                    op0=ALU.mult, op1=ALU.mult), reads=[hT_t[b], rstd_t, C], writes=[fin_t])
            for tt in range(4):
                t = 4 * b + tt
                b0 = 2 * (tt % 2)
                for c in range(KC):
                    S.op("pe", lambda c=c, b0=b0, tt=tt: nc.tensor.transpose(
                        out=PS[:, b0 + c // 4, (c % 4) * 128:(c % 4 + 1) * 128], in_=fin[:, c, tl(tt)], identity=ident_f[:]),
                        reads=[fin_t, C], writes=[PT[b0 + c // 4]])
                oi = tt % 2
                for hh in range(2):
                    evac(osb[oi][:, 512 * hh:512 * (hh + 1)], PS[:, b0 + hh, :], reads=[PT[b0 + hh]], writes=[osb_t[oi]])
                S.dma("sp", "d_o%d" % oi, out_d[s, tl(t), :], osb[oi][:, :], reads=[osb_t[oi]])

    for s in range(nseq):
        rope_tables(s)
        load_x(s)
        S.barrier()
        for l in range(NL):
            for _ in rmsnorm(l):
                pass
            attention(l)
            S.barrier()
            if not (s >= 1 and "prep" in skip):
                ssd_prep(l)
            for gp in range(2):
                if not (s >= 1 and "ssd" in skip):
                    ssd(l, gp)
            S.barrier()
            for _ in rmsnorm(2 + l):
                pass
            ffn(l)
            S.barrier()
        final_out(s)
        S.barrier()
    S.finish("sp", osb_t)
    S.finish("act", osb_t)
    return nc


def _prep_weights(inputs):
    w_in = np.asarray(inputs["w_in"], dtype=np.float32)
    w_out = np.asarray(inputs["w_out"], dtype=np.float32)
    perm = []
    for j in range(4):
        perm += list(range(64 * j, 64 * j + 64)) + list(range(64 * (j + 4), 64 * (j + 4) + 64))
    perm = np.array(perm)
    w_in2 = np.ascontiguousarray(np.concatenate([w_in[:, :, perm], w_in[:, :, 512:]], axis=2))
    w_out2 = np.ascontiguousarray(np.concatenate([w_out[:, perm, :], w_out[:, 512:, :]], axis=1))
    norms = np.ascontiguousarray(np.concatenate([np.asarray(inputs["norm_mix"], np.float32),
                                                 np.asarray(inputs["norm_ffn"], np.float32),
                                                 np.asarray(inputs["final_norm"], np.float32)[None, :]], axis=0))
    return w_in2, w_out2, norms


def make_in_maps(inputs, n_cores, nseq):
    w_in2, w_out2, norms = _prep_weights(inputs)
    x = np.asarray(inputs["x"], dtype=np.float32)
    pos = np.asarray(inputs["positions"], dtype=np.int32)
    common = {
        "w_in": w_in2, "w_out": w_out2,
        "w_gate": np.ascontiguousarray(np.asarray(inputs["w_gate"], np.float32)),
        "w_up": np.ascontiguousarray(np.asarray(inputs["w_up"], np.float32)),
        "w_down": np.ascontiguousarray(np.asarray(inputs["w_down"], np.float32)),
        "conv_w": np.ascontiguousarray(np.asarray(inputs["conv_w"], np.float32)),
        "conv_b": np.ascontiguousarray(np.asarray(inputs["conv_b"], np.float32)),
        "dt_bias": np.ascontiguousarray(np.asarray(inputs["dt_bias"], np.float32)),
        "a_log": np.ascontiguousarray(np.asarray(inputs["a_log"], np.float32)),
        "d_skip": np.ascontiguousarray(np.asarray(inputs["d_skip"], np.float32)),
        "ssm_norm": np.ascontiguousarray(np.asarray(inputs["ssm_norm"], np.float32)),
        "norms": norms,
    }
    maps = []
    for i in range(n_cores):
        m = dict(common)
        m["x"] = np.ascontiguousarray(x[i * nseq:(i + 1) * nseq])
        m["pos"] = np.ascontiguousarray(pos[i * nseq:(i + 1) * nseq])
        maps.append(m)
    return maps


def kernel(**inputs):
    nseq = 4
    nc = build(nseq)
    in_maps = make_in_maps(inputs, N_CORES, nseq)
    res = run_bass_kernel_spmd(nc, in_maps, core_ids=list(range(N_CORES)))
    out = np.concatenate([np.asarray(r["out"], dtype=np.float32) for r in res.results], axis=0)
    return out
```

```python
import math
import numpy as np
import concourse.bass as bass
import concourse.mybir as mybir
from concourse.bass_utils import run_bass_kernel_spmd

F32 = mybir.dt.float32
BF16 = mybir.dt.bfloat16
I32 = mybir.dt.int32
AF = mybir.ActivationFunctionType
ALU = mybir.AluOpType

D = 1024
T = 2048
NT = 16
KC = 8
NL = 2
FF = 2816
EPS = 1e-5
N_CORES = 8
TWO_PI = 2.0 * math.pi


class Trk:
    __slots__ = ("w", "r")

    def __init__(self):
        self.w = None
        self.r = {}


class Sched:
    def __init__(self, nc, needed=None):
        import bisect
        self._bisect = bisect
        self.nc = nc
        self.eng = {"pe": nc.tensor, "act": nc.scalar, "dve": nc.vector, "pool": nc.gpsimd, "sp": nc.sync}
        self.sem = {}
        self.cnt = {}
        self.seen = {e: {} for e in self.eng}
        self.record = needed is None
        self.needed = {e: set() for e in ("pe", "act", "dve", "pool")} if needed is None else None
        self.rank = None if needed is None else {e: sorted(v) for e, v in needed.items()}
        self.nset = needed
        for e in ("pe", "act", "dve", "pool"):
            self.sem[e] = nc.alloc_semaphore("s_" + e)
            self.cnt[e] = 0

    def dsem(self, name):
        if name not in self.sem:
            self.sem[name] = self.nc.alloc_semaphore("s_" + name)
            self.cnt[name] = 0
        return name

    def _emit_wait(self, e, f, c):
        if f in self.eng:
            if self.record:
                self.needed[f].add(c)
                val = c
            else:
                val = self._bisect.bisect_right(self.rank[f], c)
            self.eng[e].wait_ge(self.sem[f], val)
        else:
            self.eng[e].wait_ge(self.sem[f], c)

    def _wait(self, e, reads, writes):
        deps = {}
        for t in reads:
            if t.w is not None:
                deps[t.w[0]] = max(deps.get(t.w[0], 0), t.w[1])
        for t in writes:
            if t.w is not None:
                deps[t.w[0]] = max(deps.get(t.w[0], 0), t.w[1])
            for f, c in t.r.items():
                deps[f] = max(deps.get(f, 0), c)
        for f, c in deps.items():
            if f == e and e == "pe":
                continue
            if f not in self.eng:
                c = self.cnt[f]
            if c > self.seen[e].get(f, 0):
                self._emit_wait(e, f, c)
                self.seen[e][f] = c

    def op(self, e, fn, reads=(), writes=(), inc=True):
        self._wait(e, reads, writes)
        ins = fn()
        self.cnt[e] += 1
        mark = self.cnt[e]
        if self.record or mark in self.nset[e]:
            ins.then_inc(self.sem[e], 1)
        for t in reads:
            t.r[e] = max(t.r.get(e, 0), mark)
        for t in writes:
            t.w = (e, mark)
            t.r = {}
        return ins

    def dma(self, q, ds, out, in_, reads=(), writes=(), **kw):
        self.dsem(ds)
        self._wait(q, reads, writes)
        ins = self.eng[q].dma_start(out=out, in_=in_, **kw)
        ins.then_inc(self.sem[ds], 16)
        self.cnt[ds] += 16
        mark = self.cnt[ds]
        for t in reads:
            t.r[ds] = max(t.r.get(ds, 0), mark)
        for t in writes:
            t.w = (ds, mark)
            t.r = {}
        return ins

    def barrier(self):
        for e in self.eng:
            for f, c in self.cnt.items():
                if f == e and e == "pe":
                    continue
                if c > self.seen[e].get(f, 0):
                    self._emit_wait(e, f, c)
                    self.seen[e][f] = c

    def finish(self, e, trks):
        self._wait(e, (), trks)


def build(nseq=4, dump=None, skip=()):
    needed = _build(nseq, skip, None)
    return _build(nseq, skip, needed)


def _build(nseq, skip, needed):
    nc = bass.Bass("TRN2", target_bir_lowering=False)
    S = Sched(nc, needed)

    def din(name, shape, dt=F32):
        return nc.dram_tensor(name, list(shape), dt, kind="ExternalInput").ap()

    x_d = din("x", [nseq, T, D])
    pos_d = din("pos", [nseq, T], I32)
    w_in_d = din("w_in", [NL, D, 3344])
    w_out_d = din("w_out", [NL, 1536, D])
    w_gate_d = din("w_gate", [NL, D, FF])
    w_up_d = din("w_up", [NL, D, FF])
    w_down_d = din("w_down", [NL, FF, D])
    conv_w_d = din("conv_w", [NL, 4, 1536])
    conv_b_d = din("conv_b", [NL, 1536])
    dt_bias_d = din("dt_bias", [NL, 16])
    a_log_d = din("a_log", [NL, 16])
    d_skip_d = din("d_skip", [NL, 16])
    ssm_norm_d = din("ssm_norm", [NL, 1024])
    norms_d = din("norms", [5, 1024])
    out_d = nc.dram_tensor("out", [nseq, T, D], F32, kind="ExternalOutput").ap()

    dbg_outs = {}

    def sb(name, shape, dt):
        return nc.alloc_sbuf_tensor(name, list(shape), dt)

    PS = nc.alloc_psum_tensor("PS", [128, 6, 512], F32)
    PB = nc.alloc_psum_tensor("PB", [128, 2, 1024], BF16)
    PT = [Trk() for _ in range(6)]
    PBT = [Trk(), Trk()]

    hT = sb("hT", [128, KC, T], F32)
    hT_t = [Trk() for _ in range(4)]
    hnT = sb("hnT", [128, KC, T], BF16)
    hnT_t = [Trk() for _ in range(4)]

    ones_bf = sb("ones_bf", [128, 128], BF16)
    zeros_bf = sb("zeros_bf", [128, 128], BF16)
    e0_bf = sb("e0_bf", [128, 128], BF16)
    ident_bf = sb("ident_bf", [128, 128], BF16)
    ident_f = sb("ident_f", [128, 128], F32)
    ones_f = sb("ones_f", [128, 128], F32)
    tri_f = sb("tri_f", [128, 128], F32)
    mdiag = sb("mdiag", [128, 128], BF16)
    mprev = sb("mprev", [128, 128], BF16)
    m3 = sb("m3", [128, 4, 32], BF16)
    invf = sb("invf", [128, 8], F32)
    pcol = sb("pcol", [128, 12, 10], F32)
    ncol = sb("ncol", [128, 8, 5], F32)
    bc16 = sb("bc16", [128, 3, NL * 16], F32)
    epsc = sb("epsc", [128, 2], F32)
    cosb = sb("cosb", [128, 16, 8], F32)
    sinb = sb("sinb", [128, 16, 8], F32)
    rope_t = Trk()
    NEGm = sb("NEGm", [128, 128], BF16)
    C = Trk()

    arena_base = (nc.sbuf_base + 63) // 64 * 64
    arena_size = nc.sbuf_top - arena_base - 64
    nc.alloc_sbuf_tensor("arena", [128, (nc.sbuf_top - nc.sbuf_base - 64) // 4], F32)
    ar = {"off": 0}
    DTS = {F32: 4, BF16: 2, I32: 4}

    def region():
        ar["off"] = 0

    def al(name, shape, dt, at=None):
        n = DTS[dt]
        for d in shape[1:]:
            n *= d
        if at is None:
            off = (ar["off"] + 31) // 32 * 32
            ar["off"] = off + n
        else:
            off = at
        assert off + n <= arena_size, (name, off, n, arena_size)
        return nc.alloc_sbuf_tensor_at(name, list(shape), dt, offset=arena_base + off)

    region()
    sq = [al("sq0", [128, KC, 512], BF16)]
    sq_t = [Trk()]
    rstd = al("rstd", [128, 512], F32)
    rstd_t = Trk()
    norm_end = ar["off"]
    wq = al("wq", [128, KC, 768], BF16)
    wq_t = Trk()
    wo_a = al("wo_a", [128, 4, D], BF16)
    wo_a_t = Trk()
    qT = al("qT", [128, 4, T], BF16)
    qT_t = [Trk() for _ in range(NT)]
    kT = al("kT", [128, T], BF16)
    kT_t = [Trk() for _ in range(NT)]
    va_off = (ar["off"] + 31) // 32 * 32
    vaug = [al("vaug%d" % i, [128, 16, 2, 128], BF16) for i in range(3)]
    vaug_t = [[Trk() for _ in range(16)] for _ in range(3)]
    prow = al("prow", [10, 1536], F32, at=va_off)
    nrow = al("nrow", [5, 1024], F32, at=va_off + 6144)
    qk_f = al("qk_f", [128, 640], F32)
    qk_f_t = Trk()
    rtmp = al("rtmp", [128, 4, 10, 8], F32)
    rtmp_t = Trk()
    qkb = al("qkb", [128, 640], BF16)
    qkb_t = Trk()
    pT = [al("pT%d" % i, [128, 512], BF16) for i in range(3)]
    pT_t = [Trk() for _ in pT]
    rd_off = (ar["off"] + 31) // 32 * 32
    rd = al("rd", [128, 4, 512], F32)
    rd_t = Trk()
    xin = [al("xin%d" % i, [128, D], F32, at=rd_off + 4096 * i) for i in range(2)]
    xin_t = [Trk() for _ in xin]
    attnT = al("attnT", [128, 4, 512], BF16)
    attnT_t = Trk()
    posi = al("posi", [16, 128], I32)
    posf = al("posf", [16, 128], F32)
    angt = al("angt", [128, 4, 16, 8], F32)
    angi = al("angi", [128, 16, 8], I32)

    region()
    wx = al("wx", [128, KC, 768], BF16)
    wx_t = Trk()
    wz = al("wz", [128, KC, 512], BF16)
    wz_t = Trk()
    wo_s = al("wo_s", [128, 4, D], BF16)
    wo_s_t = Trk()
    wdt = al("wdt", [128, KC, 16], BF16)
    wdt_t = Trk()
    dg = al("dg", [128, 6, 4, 128], BF16)
    dg_t = Trk()
    xraw = al("xraw", [128, 6, 3 + 512], BF16)
    xraw_t = [Trk() for _ in range(6)]
    bct = [al("bct%d" % i, [128, 2, 512], BF16) for i in range(2)]
    bct_t = [Trk() for _ in bct]
    zs = al("zs", [128, 4, 512], BF16)
    zs_t = [Trk() for _ in range(4)]
    x_f = al("x_f", [128, 4, 512], BF16)
    x_f_t = [Trk() for _ in range(4)]
    B_tm = al("B_tm", [128, 4, 128], BF16)
    B_tm_t = Trk()
    smA = al("smA", [128, 10, 256], F32)
    smA_t = Trk()
    ahl = al("ahl", [128, 2, 256], BF16)
    aTr = [al("aTr%d" % i, [128, 2, 8, 128], BF16) for i in range(2)]
    aTr_t = [Trk() for _ in range(2)]
    arg = al("arg", [128, 8, 128], F32)
    arg_t = Trk()
    Lt = [al("Lt%d" % i, [128, 8, 128], BF16) for i in range(2)]
    Lt_t = [Trk() for _ in range(2)]
    cbb = [al("cbb%d" % i, [128, 128], BF16) for i in range(2)]
    cbb_t = [Trk() for _ in range(2)]
    mT = [al("mT%d" % i, [128, 8, 128], BF16) for i in range(2)]
    mT_t = [Trk() for _ in range(2)]
    xgd = [al("xgd%d" % i, [128, 3, 512], BF16) for i in range(2)]
    xgd_t = [Trk() for _ in range(2)]
    hst = al("hst", [128, 512], F32)
    hst_t = Trk()
    hbf = al("hbf", [128, 512], BF16)
    hbf_t = Trk()
    ya = al("ya", [128, 512], F32)
    ya_t = Trk()
    yjunk = al("yjunk", [128, 512], BF16)
    ysc = al("ysc", [128, 4], F32)
    ysc_t = Trk()
    yn = al("yn", [128, 512], BF16)
    yn_t = Trk()
    yT = al("yT", [128, 4, 512], BF16)
    yT_t = Trk()
    cbrow = al("cbrow", [128, 640], BF16)
    cbrow_t = Trk()
    ssmw = al("ssmw", [128, 512], F32)
    ssmw_t = Trk()

    region()
    ar["off"] = norm_end
    wg_off = (ar["off"] + 31) // 32 * 32
    wg = [al("wg%d" % i, [128, KC, 512], BF16) for i in range(2)]
    wu = [al("wu%d" % i, [128, KC, 512], BF16) for i in range(2)]
    wd = [al("wd%d" % i, [128, 4, D], BF16) for i in range(2)]
    wg_t = [Trk() for _ in range(2)]
    wu_t = [Trk() for _ in range(2)]
    wd_t = [Trk() for _ in range(2)]
    sg = [al("sg%d" % i, [128, 512], F32) for i in range(2)]
    sg_t = [Trk() for _ in sg]
    actT = al("actT", [128, 4, T], BF16)
    actT_t = [[Trk() for _ in range(4)] for _ in range(4)]
    fin = al("fin", [128, KC, 512], F32, at=wg_off)
    fin_t = Trk()
    osb = [al("osb%d" % i, [128, D], F32) for i in range(2)]
    osb_t = [Trk() for _ in osb]

    def cop(e, fn):
        S.op(e, fn, writes=[C])

    cop("dve", lambda: nc.vector.memset(ones_bf[:], 1.0))
    cop("dve", lambda: nc.vector.memset(zeros_bf[:], 0.0))
    cop("dve", lambda: nc.vector.memset(ones_f[:], 1.0))
    cop("dve", lambda: nc.vector.memset(e0_bf[:], 0.0))
    cop("dve", lambda: nc.vector.memset(e0_bf[0:1, :], 1.0))
    cop("dve", lambda: nc.vector.memset(epsc[:, 0:1], EPS))
    cop("dve", lambda: nc.vector.memset(epsc[:, 1:2], 1.0))
    for i in range(8):
        v = 500000.0 ** (-(2.0 * i) / 16.0) / TWO_PI
        cop("dve", lambda i=i, v=v: nc.vector.memset(invf[:, i:i + 1], v))

    def aff(out, in_, pattern, cm, base, op):
        S.op("pool", lambda: nc.gpsimd.affine_select(out=out, in_=in_, pattern=pattern, compare_op=op, fill=0.0,
                                                     base=base, channel_multiplier=cm), reads=[C], writes=[C])

    aff(tri_f[:], ones_f[:], [[1, 128]], -1, 0, ALU.is_ge)
    aff(ident_f[:], ones_f[:], [[1, 128]], -1, 0, ALU.is_equal)
    aff(mdiag[:], ones_bf[:], [[1, 128]], -1, 0, ALU.is_ge)
    aff(mprev[:], ones_bf[:], [[-1, 128]], 1, 0, ALU.is_ge)
    aff(ident_bf[:], ones_bf[:], [[1, 128]], -1, 0, ALU.is_equal)
    for g in range(4):
        aff(m3[:, g, :], ones_bf[:, 0:32], [[1, 32]], -1, 32 * g, ALU.is_ge)
    S.op("dve", lambda: nc.vector.tensor_tensor(out=NEGm[:], in0=mprev[:], in1=ident_bf[:], op=ALU.subtract), reads=[C], writes=[C])
    S.op("dve", lambda: nc.vector.tensor_scalar(out=NEGm[:], in0=NEGm[:], scalar1=-30000.0, scalar2=None, op0=ALU.mult),
         reads=[C], writes=[C])

    S.dma("sp", "d_c", prow[0:8, :], conv_w_d.rearrange("l k c -> (l k) c"), writes=[C])
    S.dma("sp", "d_c", prow[8:10, :], conv_b_d, writes=[C])
    S.dma("sp", "d_c", nrow[:, :], norms_d, writes=[C])
    S.dma("sp", "d_c", bc16[:, 0, :], dt_bias_d.rearrange("l n -> (l n)").partition_broadcast(128), writes=[C])
    S.dma("sp", "d_c", bc16[:, 1, :], a_log_d.rearrange("l n -> (l n)").partition_broadcast(128), writes=[C])
    S.dma("sp", "d_c", bc16[:, 2, :], d_skip_d.rearrange("l n -> (l n)").partition_broadcast(128), writes=[C])
    S.op("act", lambda: nc.scalar.activation(out=bc16[:, 1, :], in_=bc16[:, 1, :], func=AF.Exp), reads=[C], writes=[C])
    S.op("dve", lambda: nc.vector.tensor_scalar(out=bc16[:, 1, :], in0=bc16[:, 1, :], scalar1=-1.0, scalar2=None,
                                                op0=ALU.mult), reads=[C], writes=[C])
    for c in range(12):
        S.op("pe", lambda c=c: nc.tensor.transpose(out=PS[:, 0, c * 16:c * 16 + 10], in_=prow[0:10, c * 128:(c + 1) * 128],
                                                   identity=ident_f[0:10, 0:10]), reads=[C], writes=[PT[0]])
    S.op("dve", lambda: nc.vector.tensor_copy(out=pcol[:], in_=PS[:, 0, 0:192].rearrange("p (c k) -> p c k", k=16)[:, :, 0:10]),
         reads=[PT[0]], writes=[C])
    for c in range(8):
        S.op("pe", lambda c=c: nc.tensor.transpose(out=PS[:, 1, c * 8:c * 8 + 5], in_=nrow[0:5, c * 128:(c + 1) * 128],
                                                   identity=ident_f[0:5, 0:5]), reads=[C], writes=[PT[1]])
    S.op("dve", lambda: nc.vector.tensor_copy(out=ncol[:], in_=PS[:, 1, 0:64].rearrange("p (c k) -> p c k", k=8)[:, :, 0:5]),
         reads=[PT[1]], writes=[C])
    S.barrier()

    rr = {"ev": 0}

    def evac(out, in_, reads, writes, scale=None):
        rr["ev"] ^= 1
        if scale is not None or rr["ev"]:
            if scale is None:
                S.op("act", lambda: nc.scalar.activation(func=AF.Copy, out=out, in_=in_), reads=reads, writes=writes)
            else:
                S.op("act", lambda: nc.scalar.activation(func=AF.Identity, out=out, in_=in_, scale=scale), reads=reads, writes=writes)
        else:
            S.op("dve", lambda: nc.vector.tensor_copy(out=out, in_=in_), reads=reads, writes=writes)

    def blk(b):
        return slice(512 * b, 512 * (b + 1))

    def tl(t):
        return slice(128 * t, 128 * (t + 1))

    def rmsnorm(widx, final=False):
        for b in range(4):
            S.op("act", lambda b=b: nc.scalar.activation(out=sq[0][:], in_=hT[:, :, blk(b)], func=AF.Square),
                 reads=[hT_t[b]], writes=[sq_t[0]])
            bank = 4 + (b % 2)
            for c in range(KC):
                S.op("pe", lambda c=c, bank=bank: nc.tensor.matmul(PS[:, bank, :], lhsT=ones_bf[:], rhs=sq[0][:, c, :],
                                                                  start=(c == 0), stop=(c == KC - 1)),
                     reads=[sq_t[0], C], writes=[PT[bank]], inc=(c == KC - 1))
            S.op("act", lambda bank=bank: nc.scalar.activation(out=rstd[:], in_=PS[:, bank, :], func=AF.Ln,
                                                               scale=1.0 / D, bias=epsc[:, 0:1]),
                 reads=[PT[bank], C], writes=[rstd_t])
            S.op("act", lambda: nc.scalar.activation(out=rstd[:], in_=rstd[:], func=AF.Exp, scale=-0.5),
                 reads=[rstd_t], writes=[rstd_t])
            if not final:
                for c in range(KC):
                    S.op("dve", lambda c=c, b=b: nc.vector.scalar_tensor_tensor(
                        out=hnT[:, c, blk(b)], in0=hT[:, c, blk(b)], scalar=ncol[:, c, widx:widx + 1], in1=rstd[:],
                        op0=ALU.mult, op1=ALU.mult), reads=[hT_t[b], rstd_t, C], writes=[hnT_t[b]])
            else:
                yield b

    def load_w(dst, dst_t, src_rows_fn, nchunks, ds, q="pool"):
        for c in range(nchunks):
            S.dma(q, ds, dst[:, c, :], src_rows_fn(c), writes=[dst_t])

    def rope_tables(s):
        S.dma("sp", "d_pos", posi[:, :], pos_d[s].rearrange("(t p) -> t p", p=128), writes=[rope_t])
        S.op("dve", lambda: nc.vector.tensor_copy(out=posf[:], in_=posi[:]), reads=[rope_t], writes=[rope_t])
        S.op("pe", lambda: nc.tensor.transpose(out=PS[:, 5, 0:16], in_=posf[0:16, :], identity=ident_f[0:16, 0:16]),
             reads=[rope_t, C], writes=[PT[5]])
        S.op("dve", lambda: nc.vector.tensor_tensor(out=angt[:, 0], in0=PS[:, 5, 0:16].unsqueeze(2).broadcast_to([128, 16, 8]),
                                                    in1=invf[:].unsqueeze(1).broadcast_to([128, 16, 8]), op=ALU.mult),
             reads=[PT[5], C], writes=[rope_t])
        for which, shift, dst in ((0, 0.0, sinb), (1, 0.25, cosb)):
            f = angt[:, 1]
            if shift != 0.0:
                S.op("dve", lambda: nc.vector.tensor_scalar(out=angt[:, 1], in0=angt[:, 0], scalar1=shift, scalar2=None,
                                                            op0=ALU.add), reads=[rope_t], writes=[rope_t])
            else:
                S.op("dve", lambda: nc.vector.tensor_copy(out=angt[:, 1], in_=angt[:, 0]), reads=[rope_t], writes=[rope_t])
            S.op("dve", lambda: nc.vector.tensor_copy(out=angi[:], in_=f), reads=[rope_t], writes=[rope_t])
            S.op("dve", lambda: nc.vector.tensor_copy(out=angt[:, 2], in_=angi[:]), reads=[rope_t], writes=[rope_t])
            S.op("dve", lambda: nc.vector.tensor_tensor(out=angt[:, 1], in0=angt[:, 1], in1=angt[:, 2], op=ALU.subtract),
                 reads=[rope_t], writes=[rope_t])
            S.op("dve", lambda: nc.vector.tensor_single_scalar(out=angt[:, 2], in_=angt[:, 1], scalar=0.5, op=ALU.is_gt),
                 reads=[rope_t], writes=[rope_t])
            S.op("dve", lambda: nc.vector.tensor_tensor(out=angt[:, 1], in0=angt[:, 1], in1=angt[:, 2], op=ALU.subtract),
                 reads=[rope_t], writes=[rope_t])
            S.op("dve", lambda: nc.vector.tensor_single_scalar(out=angt[:, 2], in_=angt[:, 1], scalar=-0.5, op=ALU.is_lt),
                 reads=[rope_t], writes=[rope_t])
            S.op("dve", lambda: nc.vector.tensor_tensor(out=angt[:, 1], in0=angt[:, 1], in1=angt[:, 2], op=ALU.add),
                 reads=[rope_t], writes=[rope_t])
            S.op("act", lambda dst=dst: nc.scalar.activation(out=dst[:], in_=angt[:, 1], func=AF.Sin,
                                                             scale=TWO_PI * (1.0 - 2e-6)),
                 reads=[rope_t], writes=[rope_t])

    def load_x(s):
        for t in range(NT):
            sl = t % 2
            S.dma("sp", "d_xin%d" % sl, xin[sl][:, :], x_d[s, tl(t), :], writes=[xin_t[sl]])
            b0 = 2 * (t % 2)
            for c in range(KC):
                S.op("pe", lambda c=c, b0=b0, sl=sl: nc.tensor.transpose(
                    out=PS[:, b0 + c // 4, (c % 4) * 128:(c % 4 + 1) * 128], in_=xin[sl][:, c * 128:(c + 1) * 128],
                    identity=ident_f[:]), reads=[xin_t[sl], C], writes=[PT[b0 + c // 4]])
            for hh in range(2):
                evac(hT[:, 4 * hh:4 * hh + 4, tl(t)], PS[:, b0 + hh, :].rearrange("p (c t) -> p c t", t=128),
                     reads=[PT[b0 + hh]], writes=[hT_t[t // 4]])

    def attention(l):
        load_w(wq, wq_t, lambda c: w_in_d[l, c * 128:(c + 1) * 128, 0:768], KC, "d_wq")
        load_w(wo_a, wo_a_t, lambda c: w_out_d[l, c * 128:(c + 1) * 128, :], 4, "d_woa")
        for a in range(3):
            S.op("pool", lambda a=a: nc.gpsimd.memset(vaug[a][:], 1.0), writes=vaug_t[a])
        for t in range(NT):
            b0 = 2 * (t % 2)
            for c in range(KC):
                S.op("pe", lambda c=c, b0=b0, t=t: nc.tensor.matmul(PS[:, b0, :], lhsT=hnT[:, c, tl(t)], rhs=wq[:, c, 0:512],
                                                                    start=(c == 0), stop=(c == KC - 1)),
                     reads=[hnT_t[t // 4], wq_t], writes=[PT[b0]], inc=(c == KC - 1))
            for c in range(KC):
                S.op("pe", lambda c=c, b0=b0, t=t: nc.tensor.matmul(PS[:, b0 + 1, 0:256], lhsT=hnT[:, c, tl(t)],
                                                                    rhs=wq[:, c, 512:768], start=(c == 0), stop=(c == KC - 1)),
                     reads=[hnT_t[t // 4], wq_t], writes=[PT[b0 + 1]], inc=(c == KC - 1))
            S.op("act", lambda b0=b0: nc.scalar.activation(func=AF.Identity, out=qk_f[:, 0:512], in_=PS[:, b0, :], scale=0.125),
                 reads=[PT[b0]], writes=[qk_f_t])
            S.op("act", lambda b0=b0: nc.scalar.activation(func=AF.Copy, out=qk_f[:, 512:640], in_=PS[:, b0 + 1, 0:128]),
                 reads=[PT[b0 + 1]], writes=[qk_f_t])
            S.op("dve", lambda b0=b0, t=t: nc.vector.tensor_copy(out=vaug[0][:, t, 0, 0:64], in_=PS[:, b0 + 1, 128:192]),
                 reads=[PT[b0 + 1]], writes=[vaug_t[0][t]])
            S.op("dve", lambda b0=b0, t=t: nc.vector.tensor_copy(out=vaug[0][:, t, 1, 64:128], in_=PS[:, b0 + 1, 192:256]),
                 reads=[PT[b0 + 1]], writes=[vaug_t[0][t]])
            qv = qk_f[:].rearrange("p (h c) -> p h c", c=64)
            t1 = qv[:, :, 0:8]
            t2 = qv[:, :, 8:16]
            cs = cosb[:, t, :].unsqueeze(1).broadcast_to([128, 10, 8])
            sn = sinb[:, t, :].unsqueeze(1).broadcast_to([128, 10, 8])
            for i, (a_, b_) in enumerate(((t1, cs), (t2, sn), (t2, cs), (t1, sn))):
                S.op("dve", lambda i=i, a_=a_, b_=b_: nc.vector.tensor_tensor(out=rtmp[:, i], in0=a_, in1=b_, op=ALU.mult),
                     reads=[qk_f_t, rope_t], writes=[rtmp_t])
            S.op("dve", lambda: nc.vector.tensor_tensor(out=t1, in0=rtmp[:, 0], in1=rtmp[:, 1], op=ALU.subtract),
                 reads=[rtmp_t], writes=[qk_f_t])
            S.op("dve", lambda: nc.vector.tensor_tensor(out=t2, in0=rtmp[:, 2], in1=rtmp[:, 3], op=ALU.add),
                 reads=[rtmp_t], writes=[qk_f_t])
            S.op("dve", lambda: nc.vector.tensor_copy(out=qkb[:], in_=qk_f[:]), reads=[qk_f_t], writes=[qkb_t])
            pb = t % 2
            for j in range(5):
                S.op("pe", lambda j=j, pb=pb: nc.tensor.transpose(out=PB[:, pb, j * 128:(j + 1) * 128],
                                                                  in_=qkb[:, j * 128:(j + 1) * 128], identity=ident_bf[:]),
                     reads=[qkb_t, C], writes=[PBT[pb]])
            S.op("act", lambda pb=pb, t=t: nc.scalar.activation(func=AF.Copy, out=qT[:, :, tl(t)],
                                                          in_=PB[:, pb, 0:512].rearrange("p (j t) -> p j t", t=128)),
                 reads=[PBT[pb]], writes=[qT_t[t]])
            S.op("dve", lambda pb=pb, t=t: nc.vector.tensor_copy(out=kT[:, tl(t)], in_=PB[:, pb, 512:640]),
                 reads=[PBT[pb]], writes=[kT_t[t]])
        for a, dil in ((1, 4), (2, 16)):
            for grp in range(4):
                bank = 4 + (grp % 2)
                for i in range(4):
                    bi = grp * 4 + i
                    if dil == 4:
                        n, r = bi // 4, bi % 4
                        tsl = slice(512 * n + r, 512 * (n + 1), 4)
                        hts = [hnT_t[n]]
                    else:
                        tsl = slice(bi, T, 16)
                        hts = hnT_t
                    for c in range(KC):
                        S.op("pe", lambda c=c, i=i, bank=bank, tsl=tsl: nc.tensor.matmul(
                            PS[:, bank, i * 128:(i + 1) * 128], lhsT=hnT[:, c, tsl], rhs=wq[:, c, 640:768],
                            start=(c == 0), stop=(c == KC - 1)), reads=hts + [wq_t], writes=[PT[bank]], inc=(c == KC - 1))
                pv = PS[:, bank, :].rearrange("p (i c) -> p i c", c=128)
                wts = [vaug_t[a][grp * 4 + i] for i in range(4)]
                S.op("act", lambda a=a, grp=grp, pv=pv: nc.scalar.activation(func=AF.Copy, out=vaug[a][:, grp * 4:grp * 4 + 4, 0, 0:64],
                                                                       in_=pv[:, :, 0:64]), reads=[PT[bank]], writes=wts)
                S.op("dve", lambda a=a, grp=grp, pv=pv: nc.vector.tensor_copy(out=vaug[a][:, grp * 4:grp * 4 + 4, 1, 64:128],
                                                                              in_=pv[:, :, 64:128]), reads=[PT[bank]], writes=wts)
        state = {"s": 0, "p": 0}

        def score_block(kv, lhs_sl, q_sl_list, mask_ap, k_trks, q_trks):
            rows = slice(64 * kv, 64 * kv + 64)
            sbank = 4 + state["s"]
            state["s"] ^= 1
            pi = state["p"]
            state["p"] = (pi + 1) % 3
            n = len(q_sl_list)
            for i, (off, ncol, qsl) in enumerate(q_sl_list):
                S.op("pe", lambda off=off, ncol=ncol, qsl=qsl, lsl=lhs_sl[i]: nc.tensor.matmul(
                    PS[:, sbank, off:off + 4 * ncol].rearrange("p (g q) -> p g q", q=ncol),
                    lhsT=kT[rows, lsl], rhs=qT[rows, :, qsl], start=True, stop=True),
                    reads=k_trks + q_trks, writes=[PT[sbank]], inc=(i == n - 1))
            S.op("act", lambda: nc.scalar.activation(out=pT[pi][:], in_=PS[:, sbank, :], func=AF.Exp),
                 reads=[PT[sbank]], writes=[pT_t[pi]])
            S.op("dve", lambda: nc.vector.tensor_tensor(out=pT[pi][:].rearrange("p (a q) -> p a q", q=mask_ap.shape[-1]),
                                                        in0=pT[pi][:].rearrange("p (a q) -> p a q", q=mask_ap.shape[-1]),
                                                        in1=mask_ap, op=ALU.mult),
                 reads=[pT_t[pi], C], writes=[pT_t[pi]])
            return pi

        OT = [PT[n] for n in range(4)]

        def zero_init():
            for n in range(4):
                S.op("pe", lambda n=n: nc.tensor.matmul(PS[:, n, :], lhsT=zeros_bf[:], rhs=hnT[:, 0, 0:512],
                                                        start=True, stop=False), reads=[hnT_t[0], C], writes=[PT[n]])

        def normalize(kv):
            nrows = slice(64 * kv, 64 * kv + 64)
            drows = slice(64 * (1 - kv), 64 * (1 - kv) + 64)
            Ov = PS[:, 0:4, :]
            S.op("act", lambda: nc.scalar.activation(out=rd[nrows], in_=Ov[drows], func=AF.Ln), reads=OT, writes=[rd_t])
            S.op("act", lambda: nc.scalar.activation(out=rd[nrows], in_=rd[nrows], func=AF.Exp, scale=-1.0),
                 reads=[rd_t], writes=[rd_t])
            S.op("dve", lambda: nc.vector.tensor_tensor(
                out=attnT[nrows].rearrange("p g (n t) -> p n g t", t=128),
                in0=Ov[nrows].rearrange("p n (g t) -> p n g t", t=128),
                in1=rd[nrows].rearrange("p n (g t) -> p n g t", t=128), op=ALU.mult),
                reads=OT + [rd_t], writes=[attnT_t])

        def out_proj(G):
            for dc in range(KC):
                bank = 4 + (dc % 2)
                for j in range(4):
                    S.op("pe", lambda j=j, dc=dc, bank=bank: nc.tensor.matmul(
                        PS[:, bank, :], lhsT=wo_a[:, j, dc * 128:(dc + 1) * 128], rhs=attnT[:, j, :],
                        start=(j == 0), stop=(j == 3)), reads=[wo_a_t, attnT_t], writes=[PT[bank]], inc=(j == 3))
                S.op("dve", lambda dc=dc, bank=bank, G=G: nc.vector.tensor_tensor(
                    out=hT[:, dc, blk(G)], in0=hT[:, dc, blk(G)], in1=PS[:, bank, :], op=ALU.add),
                    reads=[PT[bank], hT_t[G]], writes=[hT_t[G]])

        blocks = []
        for G in range(4):
            for kv in range(2):
                first = [True]
                grp = []
                for n in range(4):
                    qb = 4 * G + n
                    for kb in (qb - 1, qb):
                        if kb < 0:
                            continue
                        mask = (mdiag if kb == qb else mprev)[:].unsqueeze(1).broadcast_to([128, 4, 128])

                        def sc(kv=kv, kb=kb, qb=qb, mask=mask):
                            return score_block(kv, [tl(kb)], [(0, 128, tl(qb))], mask, [kT_t[kb]], [qT_t[qb]])

                        def pv(pi, kv=kv, kb=kb, n=n):
                            S.op("pe", lambda: nc.tensor.matmul(
                                PS[:, n, :], lhsT=vaug[0][:, kb, kv, :], rhs=pT[pi][:], start=False, stop=False,
                                skip_group_check=True), reads=[pT_t[pi], vaug_t[0][kb]], writes=[PT[n]])
                        grp.append((sc, pv))
                for r in range(4):
                    qsl = slice(512 * G + r, 512 * (G + 1), 4)
                    for Gk in (G - 1, G):
                        if Gk < 0:
                            continue
                        ksl = slice(512 * Gk + r, 512 * (Gk + 1), 4)
                        mask = (mdiag if Gk == G else mprev)[:].unsqueeze(1).broadcast_to([128, 4, 128])

                        def sc(kv=kv, ksl=ksl, qsl=qsl, mask=mask, Gk=Gk, G=G):
                            return score_block(kv, [ksl], [(0, 128, qsl)], mask, kT_t[4 * Gk:4 * Gk + 4], qT_t[4 * G:4 * G + 4])

                        def pv(pi, kv=kv, Gk=Gk, r=r):
                            pvv = pT[pi][:].rearrange("p (g q) -> p g q", q=128)
                            for n in range(4):
                                S.op("pe", lambda n=n: nc.tensor.matmul(
                                    PS[:, n, :].rearrange("p (g t) -> p g t", t=128)[:, :, r:128:4],
                                    lhsT=vaug[1][:, Gk * 4 + r, kv, :], rhs=pvv[:, :, 32 * n:32 * n + 32], start=False,
                                    stop=False, skip_group_check=True), reads=[pT_t[pi], vaug_t[1][Gk * 4 + r]],
                                    writes=[PT[n]], inc=(n == 3))
                        grp.append((sc, pv))
                for r4 in range(4):
                    lhs = [slice(4 * r4 + i, T, 16) for i in range(4)]
                    qsl = [(i * 128, 32, slice(512 * G + 4 * r4 + i, 512 * (G + 1), 16)) for i in range(4)]
                    mask = m3[:, G, :].unsqueeze(1).broadcast_to([128, 16, 32])

                    def sc(kv=kv, lhs=lhs, qsl=qsl, mask=mask, G=G):
                        return score_block(kv, lhs, qsl, mask, kT_t, qT_t[4 * G:4 * G + 4])

                    def pv(pi, kv=kv, r4=r4):
                        pvv = pT[pi][:].rearrange("p (i g q) -> p i g q", g=4, q=32)
                        for i in range(4):
                            r = 4 * r4 + i
                            for n in range(4):
                                last = (r == 15)
                                S.op("pe", lambda n=n, i=i, r=r, last=last: nc.tensor.matmul(
                                    PS[:, n, :].rearrange("p (g t) -> p g t", t=128)[:, :, r:128:16],
                                    lhsT=vaug[2][:, r, kv, :], rhs=pvv[:, i, :, 8 * n:8 * n + 8], start=False, stop=last,
                                    skip_group_check=True), reads=[pT_t[pi], vaug_t[2][r]], writes=[PT[n]],
                                    inc=(i == 3 and n == 3))
                    grp.append((sc, pv))
                for bi, (sc, pv) in enumerate(grp):
                    blocks.append((sc, pv, bi == 0, bi == len(grp) - 1, G, kv))
        pend = None

        def flush(p):
            sc, pv, isfirst, islast, G, kv, pi = p
            if isfirst:
                zero_init()
            pv(pi)
            if islast:
                for n in range(4):
                    S.op("pe", lambda n=n: nc.tensor.matmul(PS[:, n, :], lhsT=zeros_bf[:], rhs=hnT[:, 0, 0:512],
                                                            start=False, stop=True), reads=[hnT_t[0], C], writes=[PT[n]],
                         inc=(n == 3))
                normalize(kv)
                if kv == 1:
                    out_proj(G)

        for (sc, pv, isfirst, islast, G, kv) in blocks:
            pi = sc()
            if pend is not None:
                flush(pend)
            pend = (sc, pv, isfirst, islast, G, kv, pi)
        flush(pend)

    U_, T1_, DTV_, W2_, DSK_, A_, NACS_, EA_, EAT_, DEC_ = range(10)

    def ssd_prep(l):
        sa = lambda i: smA[:, i, :]
        v3 = lambda ap: ap.rearrange("p (t h) -> p t h", h=16)
        bc = lambda j: bc16[:, j, l * 16:l * 16 + 16].unsqueeze(1).broadcast_to([128, 16, 16])
        load_w(wdt, wdt_t, lambda c: w_in_d[l, c * 128:(c + 1) * 128, 3328:3344], KC, "d_wdt")
        for t in range(NT):
            for c in range(KC):
                S.op("pe", lambda c=c, t=t: nc.tensor.matmul(PS[:, 2, t * 16:(t + 1) * 16], lhsT=hnT[:, c, tl(t)], rhs=wdt[:, c, :],
                                                             start=(c == 0), stop=(c == KC - 1)),
                     reads=[hnT_t[t // 4], wdt_t], writes=[PT[2]], inc=(c == KC - 1))
        S.op("dve", lambda: nc.vector.tensor_tensor(out=v3(sa(U_)), in0=PS[:, 2, 0:256].rearrange("p (t h) -> p t h", h=16),
                                                    in1=bc(0), op=ALU.add),
             reads=[PT[2], C], writes=[smA_t])
        S.op("act", lambda: nc.scalar.activation(out=sa(T1_), in_=sa(U_), func=AF.Abs), reads=[smA_t], writes=[smA_t])
        S.op("act", lambda: nc.scalar.activation(out=sa(T1_), in_=sa(T1_), func=AF.Exp, scale=-1.0),
             reads=[smA_t], writes=[smA_t])
        S.op("act", lambda: nc.scalar.activation(out=sa(T1_), in_=sa(T1_), func=AF.Ln, bias=epsc[:, 1:2]),
             reads=[smA_t, C], writes=[smA_t])
        S.op("dve", lambda: nc.vector.tensor_single_scalar(out=sa(U_), in_=sa(U_), scalar=0.0, op=ALU.max),
             reads=[smA_t], writes=[smA_t])
        S.op("dve", lambda: nc.vector.tensor_tensor(out=sa(DTV_), in0=sa(U_), in1=sa(T1_), op=ALU.add),
             reads=[smA_t], writes=[smA_t])
        S.op("dve", lambda: nc.vector.tensor_tensor(out=v3(sa(A_)), in0=v3(sa(DTV_)), in1=bc(1), op=ALU.mult),
             reads=[smA_t, C], writes=[smA_t])
        S.op("dve", lambda: nc.vector.tensor_copy(out=v3(sa(DSK_)), in_=bc(2)), reads=[C], writes=[smA_t])
        S.op("pe", lambda: nc.tensor.matmul(PS[:, 0, 0:256], lhsT=tri_f[:], rhs=sa(A_), start=True, stop=True),
             reads=[smA_t, C], writes=[PT[0]])
        S.op("pe", lambda: nc.tensor.matmul(PS[:, 1, 0:256], lhsT=ones_f[:], rhs=sa(A_), start=True, stop=True),
             reads=[smA_t, C], writes=[PT[1]])
        S.op("dve", lambda: nc.vector.tensor_scalar(out=sa(NACS_), in0=PS[:, 0, 0:256], scalar1=-1.0, scalar2=None,
                                                    op0=ALU.mult), reads=[PT[0]], writes=[smA_t])
        S.op("act", lambda: nc.scalar.activation(out=sa(EA_), in_=PS[:, 0, 0:256], func=AF.Exp), reads=[PT[0]], writes=[smA_t])
        S.op("act", lambda: nc.scalar.activation(out=sa(EAT_), in_=PS[:, 1, 0:256], func=AF.Exp), reads=[PT[1]], writes=[smA_t])
        S.op("dve", lambda: nc.vector.tensor_tensor(out=sa(DEC_), in0=PS[:, 1, 0:256], in1=sa(NACS_), op=ALU.add),
             reads=[PT[1], smA_t], writes=[smA_t])
        S.op("act", lambda: nc.scalar.activation(out=sa(DEC_), in_=sa(DEC_), func=AF.Exp), reads=[smA_t], writes=[smA_t])
        S.op("dve", lambda: nc.vector.tensor_tensor(out=sa(W2_), in0=sa(DTV_), in1=sa(DEC_), op=ALU.mult),
             reads=[smA_t], writes=[smA_t])
        S.op("dve", lambda: nc.vector.tensor_copy(out=ahl[:, 0, :], in_=sa(A_)), reads=[smA_t], writes=[smA_t])
        S.op("dve", lambda: nc.vector.tensor_tensor(out=ahl[:, 1, :], in0=sa(A_), in1=ahl[:, 0, :], op=ALU.subtract),
             reads=[smA_t], writes=[smA_t])

    def ssd(l, gp):
        xo = 1792 + 512 * gp
        bo = 2816 + 128 * gp
        co = 3072 + 128 * gp
        for c in range(KC):
            rs = slice(c * 128, (c + 1) * 128)
            S.dma("pool", "d_wx", wx[:, c, 0:512], w_in_d[l, rs, xo:xo + 512], writes=[wx_t])
            S.dma("pool", "d_wx", wx[:, c, 512:640], w_in_d[l, rs, bo:bo + 128], writes=[wx_t])
            S.dma("pool", "d_wx", wx[:, c, 640:768], w_in_d[l, rs, co:co + 128], writes=[wx_t])
        load_w(wz, wz_t, lambda c: w_in_d[l, c * 128:(c + 1) * 128, 768 + 512 * gp:768 + 512 * gp + 512], KC, "d_wz")
        load_w(wo_s, wo_s_t, lambda c: w_out_d[l, 512 + 512 * gp + c * 128:512 + 512 * gp + (c + 1) * 128, :], 4, "d_wos")
        S.op("dve", lambda: nc.vector.memset(cbrow[:], 0.0), writes=[cbrow_t])
        S.dma("pool", "d_cb", cbrow[0:1, 0:512], conv_b_d[l:l + 1, 512 * gp:512 * gp + 512], writes=[cbrow_t])
        S.dma("pool", "d_cb", cbrow[0:1, 512:640], conv_b_d[l:l + 1, 1024 + 128 * gp:1024 + 128 * gp + 128], writes=[cbrow_t])
        S.dma("sp", "d_sw", ssmw[:, :], ssm_norm_d[l, 512 * gp:512 * gp + 512].partition_broadcast(128), writes=[ssmw_t])
        pcs = [gp * 4 + i for i in range(4)] + [8 + gp, 10 + gp]
        for ch in range(6):
            for k in range(4):
                S.op("dve", lambda ch=ch, k=k: nc.vector.tensor_scalar(
                    out=dg[:, ch, k, :], in0=ident_f[:], scalar1=pcol[:, pcs[ch], l * 4 + k:l * 4 + k + 1], scalar2=None,
                    op0=ALU.mult), reads=[C], writes=[dg_t])
        S.op("dve", lambda: nc.vector.memset(hst[:], 0.0), writes=[hst_t])
        S.op("dve", lambda: nc.vector.memset(hbf[:], 0.0), writes=[hbf_t])
        dsk = bc16[:, 2, l * 16 + 8 * gp:l * 16 + 8 * gp + 8]
        b8 = lambda ap: ap.unsqueeze(2).broadcast_to([128, 8, 64])
        v8 = lambda ap: ap.rearrange("p (e c) -> p e c", c=64)

        def smc(i, t):
            return smA[:, i, t * 16 + 8 * gp:t * 16 + 8 * gp + 8]

        for G in range(4):
            bc = bct[G % 2]
            bc_t = bct_t[G % 2]
            if G == 0:
                S.op("dve", lambda: nc.vector.memset(xraw[:, :, 0:3], 0.0), writes=xraw_t)
            else:
                S.op("dve", lambda: nc.vector.tensor_copy(out=xraw[:, :, 0:3], in_=xraw[:, :, 512:515]),
                     reads=xraw_t, writes=xraw_t)
            for ch in range(6):
                bank = ch % 2
                for c in range(KC):
                    S.op("pe", lambda c=c, ch=ch, bank=bank: nc.tensor.matmul(
                        PS[:, bank, :], lhsT=wx[:, c, ch * 128:(ch + 1) * 128], rhs=hnT[:, c, blk(G)],
                        start=(c == 0), stop=(c == KC - 1)), reads=[wx_t, hnT_t[G]], writes=[PT[bank]], inc=(c == KC - 1))
                evac(xraw[:, ch, 3:515], PS[:, bank, :], reads=[PT[bank]], writes=[xraw_t[ch]])
            for idx, ch in ((0, 4), (1, 5)):
                for k in range(4):
                    S.op("pe", lambda k=k, ch=ch: nc.tensor.matmul(
                        PS[:, 2, :], lhsT=dg[:, ch, k, :], rhs=xraw[:, ch, k:k + 512],
                        start=(k == 0), stop=(k == 3)), reads=[dg_t, xraw_t[ch]], writes=[PT[2]], inc=(k == 3))
                S.op("act", lambda idx=idx, ch=ch: nc.scalar.activation(
                    out=bc[:, idx, :], in_=PS[:, 2, :], func=AF.Silu, bias=pcol[:, pcs[ch], 8 + l:9 + l]),
                    reads=[PT[2], C], writes=[bc_t])
            S.op("pe", lambda: nc.tensor.matmul(PS[:, 5, :], lhsT=zeros_bf[:], rhs=hnT[:, 0, 0:512],
                                                start=True, stop=False), reads=[C, hnT_t[0]], writes=[PT[5]], inc=False)
            for tt in range(4):
                for k in range(4):
                    S.op("pe", lambda tt=tt, k=k: nc.tensor.matmul(
                        PS[:, 5, tt * 128:(tt + 1) * 128], lhsT=xraw[:, 4, 128 * tt + k:128 * tt + k + 128], rhs=dg[:, 4, k, :],
                        start=False, stop=False, skip_group_check=True), reads=[dg_t, xraw_t[4]], writes=[PT[5]],
                        inc=False)
            S.op("pe", lambda: nc.tensor.matmul(PS[:, 5, :].rearrange("p (a c) -> p a c", c=128), lhsT=e0_bf[:],
                                                rhs=cbrow[:, 512:640].unsqueeze(1).broadcast_to([128, 4, 128]),
                                                start=False, stop=True), reads=[C, cbrow_t], writes=[PT[5]])
            S.op("act", lambda: nc.scalar.activation(out=B_tm[:].rearrange("p a c -> p (a c)"), in_=PS[:, 5, :], func=AF.Silu),
                 reads=[PT[5]], writes=[B_tm_t])
            for tt in range(4):
                t = 4 * G + tt
                xb = 3 + (tt % 2)
                S.op("pe", lambda xb=xb: nc.tensor.matmul(PS[:, xb, :], lhsT=zeros_bf[:], rhs=hnT[:, 0, 0:512],
                                                          start=True, stop=False), reads=[C, hnT_t[0]], writes=[PT[xb]], inc=False)
                for i in range(4):
                    for k in range(4):
                        S.op("pe", lambda i=i, k=k, tt=tt, xb=xb: nc.tensor.matmul(
                            PS[:, xb, i * 128:(i + 1) * 128], lhsT=xraw[:, i, 128 * tt + k:128 * tt + k + 128],
                            rhs=dg[:, i, k, :], start=False, stop=False, skip_group_check=True),
                            reads=[dg_t, xraw_t[i]], writes=[PT[xb]], inc=False)
                S.op("pe", lambda xb=xb: nc.tensor.matmul(PS[:, xb, :], lhsT=e0_bf[:], rhs=cbrow[:, 0:512],
                                                          start=False, stop=True), reads=[C, cbrow_t], writes=[PT[xb]])
                S.op("act", lambda tt=tt, xb=xb: nc.scalar.activation(out=x_f[:, tt, :], in_=PS[:, xb, :], func=AF.Silu),
                     reads=[PT[xb]], writes=[x_f_t[tt]])
                zb = tt % 2
                for c in range(KC):
                    S.op("pe", lambda c=c, t=t, zb=zb: nc.tensor.matmul(PS[:, zb, :], lhsT=hnT[:, c, tl(t)], rhs=wz[:, c, :],
                                                                        start=(c == 0), stop=(c == KC - 1)),
                         reads=[hnT_t[G], wz_t], writes=[PT[zb]], inc=(c == KC - 1))
                S.op("act", lambda tt=tt, zb=zb: nc.scalar.activation(out=zs[:, tt, :], in_=PS[:, zb, :], func=AF.Silu),
                     reads=[PT[zb]], writes=[zs_t[tt]])

            def stageA(tt):
                t = 4 * G + tt
                s2 = tt % 2
                cs = slice(t * 16 + 8 * gp, t * 16 + 8 * gp + 8)
                for hb in range(2):
                    S.op("pe", lambda hb=hb: nc.tensor.matmul(PS[:, hb, :], lhsT=ident_bf[:],
                                                              rhs=NEGm[:].unsqueeze(1).broadcast_to([128, 4, 128]),
                                                              start=True, stop=False), reads=[C], writes=[PT[hb]], inc=False)
                    for e4 in range(4):
                        col = t * 16 + 8 * gp + 4 * hb + e4
                        for hl in range(2):
                            lastm = (e4 == 3 and hl == 1)
                            S.op("pe", lambda hb=hb, e4=e4, hl=hl, col=col, lastm=lastm: nc.tensor.matmul(
                                PS[:, hb, e4 * 128:(e4 + 1) * 128], lhsT=ahl[:, hl, col:col + 1].broadcast_to([128, 128]),
                                rhs=mdiag[:], start=False, stop=lastm), reads=[smA_t, C], writes=[PT[hb]], inc=lastm)
                S.op("dve", lambda: nc.vector.tensor_tensor(
                    out=arg[:], in0=PS[:, 0:2, :].rearrange("p b (e l) -> p (b e) l", l=128),
                    in1=smc(NACS_, t).unsqueeze(2).broadcast_to([128, 8, 128]), op=ALU.add),
                    reads=[PT[0], PT[1], smA_t], writes=[arg_t])
                S.op("act", lambda: nc.scalar.activation(out=Lt[s2][:], in_=arg[:], func=AF.Exp), reads=[arg_t], writes=[Lt_t[s2]])
                S.op("pe", lambda: nc.tensor.matmul(PS[:, 2, s2 * 128:(s2 + 1) * 128], lhsT=bc[:, 0, tl(tt)], rhs=bc[:, 1, tl(tt)],
                                                    start=True, stop=True), reads=[bc_t], writes=[PT[2]])
                S.op("act", lambda: nc.scalar.activation(func=AF.Copy, out=cbb[s2][:], in_=PS[:, 2, s2 * 128:(s2 + 1) * 128]),
                     reads=[PT[2]], writes=[cbb_t[s2]])
                S.op("dve", lambda: nc.vector.tensor_tensor(out=mT[s2][:], in0=Lt[s2][:],
                                                            in1=cbb[s2][:].unsqueeze(1).broadcast_to([128, 8, 128]), op=ALU.mult),
                     reads=[Lt_t[s2], cbb_t[s2]], writes=[mT_t[s2]])
                S.op("pool", lambda: nc.gpsimd.tensor_tensor(
                    out=xgd[s2][:].rearrange("p j (e c) -> p j e c", c=64),
                    in0=v8(x_f[:, tt, :]).unsqueeze(1).broadcast_to([128, 3, 8, 64]),
                    in1=smA[:, DTV_:DTV_ + 3, cs].unsqueeze(3).broadcast_to([128, 3, 8, 64]), op=ALU.mult),
                    reads=[x_f_t[tt], smA_t, C], writes=[xgd_t[s2]])

            def stageB(tt):
                t = 4 * G + tt
                s2 = tt % 2
                S.op("pe", lambda: nc.tensor.matmul(PS[:, 3, :], lhsT=zeros_bf[:], rhs=hnT[:, 0, 0:512], start=True, stop=False),
                     reads=[hnT_t[0], C], writes=[PT[3]], inc=False)
                for e in range(8):
                    S.op("pe", lambda e=e: nc.tensor.matmul(PS[:, 3, e * 64:(e + 1) * 64], lhsT=mT[s2][:, e, :],
                                                            rhs=xgd[s2][:, 0, e * 64:(e + 1) * 64], start=False, stop=False,
                                                            skip_group_check=True),
                         reads=[mT_t[s2], xgd_t[s2]], writes=[PT[3]], inc=False)
                S.op("pe", lambda: nc.tensor.matmul(PS[:, 3, :], lhsT=ident_bf[:], rhs=xgd[s2][:, 2, :], start=False, stop=True),
                     reads=[xgd_t[s2], C], writes=[PT[3]])
                S.op("pe", lambda: nc.tensor.matmul(PS[:, 4, :], lhsT=bc[:, 1, tl(tt)], rhs=hbf[:], start=True, stop=True),
                     reads=[bc_t, hbf_t], writes=[PT[4]])
                S.op("pe", lambda: nc.tensor.matmul(PS[:, 5, :], lhsT=B_tm[:, tt, :], rhs=xgd[s2][:, 1, :], start=True, stop=True),
                     reads=[B_tm_t, xgd_t[s2]], writes=[PT[5]])
                S.op("pool", lambda: nc.gpsimd.tensor_tensor(out=v8(hst[:]), in0=v8(hst[:]), in1=b8(smc(EAT_, t)), op=ALU.mult),
                     reads=[hst_t, smA_t], writes=[hst_t])
                S.op("dve", lambda: nc.vector.tensor_tensor(out=hst[:], in0=hst[:], in1=PS[:, 5, :], op=ALU.add),
                     reads=[hst_t, PT[5]], writes=[hst_t])
                S.op("act", lambda: nc.scalar.activation(func=AF.Copy, out=hbf[:], in_=hst[:]), reads=[hst_t], writes=[hbf_t])
                S.op("dve", lambda: nc.vector.tensor_tensor(out=v8(ya[:]), in0=v8(PS[:, 4, :]), in1=b8(smc(EA_, t)), op=ALU.mult),
                     reads=[PT[4], smA_t], writes=[ya_t])
                S.op("dve", lambda: nc.vector.tensor_tensor(out=ya[:], in0=ya[:], in1=PS[:, 3, :], op=ALU.add),
                     reads=[PT[3], ya_t], writes=[ya_t])
                S.op("dve", lambda: nc.vector.tensor_tensor(out=ya[:], in0=ya[:], in1=zs[:, tt, :], op=ALU.mult),
                     reads=[ya_t, zs_t[tt]], writes=[ya_t])
                S.op("act", lambda: nc.scalar.activation(out=yjunk[:], in_=ya[:], func=AF.Square, accum_out=ysc[:, 0:1]),
                     reads=[ya_t], writes=[ysc_t])
                S.op("act", lambda: nc.scalar.activation(out=ysc[:, 1:2], in_=ysc[:, 0:1], func=AF.Ln, scale=1.0 / 512,
                                                         bias=epsc[:, 0:1]), reads=[ysc_t, C], writes=[ysc_t])
                S.op("act", lambda: nc.scalar.activation(out=ysc[:, 2:3], in_=ysc[:, 1:2], func=AF.Exp, scale=-0.5),
                     reads=[ysc_t], writes=[ysc_t])
                S.op("dve", lambda: nc.vector.scalar_tensor_tensor(out=yn[:], in0=ya[:], scalar=ysc[:, 2:3], in1=ssmw[:],
                                                                   op0=ALU.mult, op1=ALU.mult),
                     reads=[ya_t, ysc_t, ssmw_t], writes=[yn_t])
                pb = tt % 2
                for j in range(4):
                    S.op("pe", lambda j=j, pb=pb: nc.tensor.transpose(out=PB[:, pb, j * 128:(j + 1) * 128],
                                                                      in_=yn[:, j * 128:(j + 1) * 128], identity=ident_bf[:]),
                         reads=[yn_t, C], writes=[PBT[pb]], inc=(j == 3))
                S.op("act", lambda pb=pb, tt=tt: nc.scalar.activation(func=AF.Copy, out=yT[:, :, tl(tt)],
                                                                      in_=PB[:, pb, 0:512].rearrange("p (j t) -> p j t", t=128)),
                     reads=[PBT[pb]], writes=[yT_t])

            stageA(0)
            for tt in range(4):
                if tt + 1 < 4:
                    stageA(tt + 1)
                stageB(tt)
            for dc in range(KC):
                bank = dc % 2
                for j in range(4):
                    S.op("pe", lambda j=j, dc=dc, bank=bank: nc.tensor.matmul(
                        PS[:, bank, :], lhsT=wo_s[:, j, dc * 128:(dc + 1) * 128], rhs=yT[:, j, :],
                        start=(j == 0), stop=(j == 3)), reads=[wo_s_t, yT_t], writes=[PT[bank]], inc=(j == 3))
                S.op("dve", lambda dc=dc, bank=bank, G=G: nc.vector.tensor_tensor(
                    out=hT[:, dc, blk(G)], in0=hT[:, dc, blk(G)], in1=PS[:, bank, :], op=ALU.add),
                    reads=[PT[bank], hT_t[G]], writes=[hT_t[G]])

    def ffn(l):
        ngrp = 6
        for fg in range(ngrp):
            nch = 4 if fg < 5 else 2
            f0 = fg * 512
            sl = fg % 2
            for c in range(KC):
                rs = slice(c * 128, (c + 1) * 128)
                S.dma("pool", "d_wg%d" % sl, wg[sl][:, c, 0:128 * nch], w_gate_d[l, rs, f0:f0 + 128 * nch], writes=[wg_t[sl]])
                S.dma("pool", "d_wu%d" % sl, wu[sl][:, c, 0:128 * nch], w_up_d[l, rs, f0:f0 + 128 * nch], writes=[wu_t[sl]])
            for j in range(nch):
                S.dma("pool", "d_wd%d" % sl, wd[sl][:, j, :], w_down_d[l, f0 + j * 128:f0 + (j + 1) * 128, :], writes=[wd_t[sl]])
            for j in range(nch):
                for b in range(4):
                    gb = 2 * (b % 2)
                    for c in range(KC):
                        S.op("pe", lambda c=c, j=j, b=b, gb=gb: nc.tensor.matmul(
                            PS[:, gb, :], lhsT=wg[sl][:, c, j * 128:(j + 1) * 128], rhs=hnT[:, c, blk(b)],
                            start=(c == 0), stop=(c == KC - 1)), reads=[wg_t[sl], hnT_t[b]], writes=[PT[gb]], inc=(c == KC - 1))
                    for c in range(KC):
                        S.op("pe", lambda c=c, j=j, b=b, gb=gb: nc.tensor.matmul(
                            PS[:, gb + 1, :], lhsT=wu[sl][:, c, j * 128:(j + 1) * 128], rhs=hnT[:, c, blk(b)],
                            start=(c == 0), stop=(c == KC - 1)), reads=[wu_t[sl], hnT_t[b]], writes=[PT[gb + 1]],
                            inc=(c == KC - 1))
                    si = b % 2
                    S.op("act", lambda gb=gb, si=si: nc.scalar.activation(out=sg[si][:], in_=PS[:, gb, :], func=AF.Silu),
                         reads=[PT[gb]], writes=[sg_t[si]])
                    S.op("dve", lambda gb=gb, si=si, j=j, b=b: nc.vector.tensor_tensor(
                        out=actT[:, j, blk(b)], in0=sg[si][:], in1=PS[:, gb + 1, :], op=ALU.mult),
                        reads=[sg_t[si], PT[gb + 1]], writes=[actT_t[j][b]])
            for dc in range(KC):
                for b in range(4):
                    bank = 4 + ((dc * 4 + b) % 2)
                    for j in range(nch):
                        S.op("pe", lambda j=j, dc=dc, b=b, bank=bank: nc.tensor.matmul(
                            PS[:, bank, :], lhsT=wd[sl][:, j, dc * 128:(dc + 1) * 128], rhs=actT[:, j, blk(b)],
                            start=(j == 0), stop=(j == nch - 1)), reads=[wd_t[sl], actT_t[j][b]], writes=[PT[bank]],
                            inc=(j == nch - 1))
                    S.op("dve", lambda dc=dc, b=b, bank=bank: nc.vector.tensor_tensor(
                        out=hT[:, dc, blk(b)], in0=hT[:, dc, blk(b)], in1=PS[:, bank, :], op=ALU.add),
                        reads=[PT[bank], hT_t[b]], writes=[hT_t[b]])

    def final_out(s):
        for b in rmsnorm(4, final=True):
            for c in range(KC):
                S.op("dve", lambda c=c, b=b: nc.vector.scalar_tensor_tensor(
                    out=fin[:, c, :], in0=hT[:, c, blk(b)], scalar=ncol[:, c, 4:5], in1=rstd[:],
                    op0=ALU.mult, op1=ALU.mult), reads=[hT_t[b], rstd_t, C], writes=[fin_t])
            for tt in range(4):
                t = 4 * b + tt
                b0 = 2 * (tt % 2)
                for c in range(KC):
                    S.op("pe", lambda c=c, b0=b0, tt=tt: nc.tensor.transpose(
                        out=PS[:, b0 + c // 4, (c % 4) * 128:(c % 4 + 1) * 128], in_=fin[:, c, tl(tt)], identity=ident_f[:]),
                        reads=[fin_t, C], writes=[PT[b0 + c // 4]])
                oi = tt % 2
                for hh in range(2):
                    evac(osb[oi][:, 512 * hh:512 * (hh + 1)], PS[:, b0 + hh, :], reads=[PT[b0 + hh]], writes=[osb_t[oi]])
                S.dma("sp", "d_o%d" % oi, out_d[s, tl(t), :], osb[oi][:, :], reads=[osb_t[oi]])

    for s in range(nseq):
        rope_tables(s)
        load_x(s)
        S.barrier()
        for l in range(NL):
            for _ in rmsnorm(l):
                pass
            attention(l)
            S.barrier()
            if not (s >= 1 and "prep" in skip):
                ssd_prep(l)
            for gp in range(2):
                if not (s >= 1 and "ssd" in skip):
                    ssd(l, gp)
            S.barrier()
            for _ in rmsnorm(2 + l):
                pass
            ffn(l)
            S.barrier()
        final_out(s)
        S.barrier()
    S.finish("sp", osb_t)
    S.finish("act", osb_t)
    if needed is None:
        return S.needed
    return nc


def _prep_weights(inputs):
    w_in = np.asarray(inputs["w_in"], dtype=np.float32)
    w_out = np.asarray(inputs["w_out"], dtype=np.float32)
    perm = []
    for j in range(4):
        perm += list(range(64 * j, 64 * j + 64)) + list(range(64 * (j + 4), 64 * (j + 4) + 64))
    perm = np.array(perm)
    w_in2 = np.ascontiguousarray(np.concatenate([w_in[:, :, perm], w_in[:, :, 512:]], axis=2))
    w_out2 = np.ascontiguousarray(np.concatenate([w_out[:, perm, :], w_out[:, 512:, :]], axis=1))
    norms = np.ascontiguousarray(np.concatenate([np.asarray(inputs["norm_mix"], np.float32),
                                                 np.asarray(inputs["norm_ffn"], np.float32),
                                                 np.asarray(inputs["final_norm"], np.float32)[None, :]], axis=0))
    return w_in2, w_out2, norms


def make_in_maps(inputs, n_cores, nseq):
    w_in2, w_out2, norms = _prep_weights(inputs)
    x = np.asarray(inputs["x"], dtype=np.float32)
    pos = np.asarray(inputs["positions"], dtype=np.int32)
    common = {
        "w_in": w_in2, "w_out": w_out2,
        "w_gate": np.ascontiguousarray(np.asarray(inputs["w_gate"], np.float32)),
        "w_up": np.ascontiguousarray(np.asarray(inputs["w_up"], np.float32)),
        "w_down": np.ascontiguousarray(np.asarray(inputs["w_down"], np.float32)),
        "conv_w": np.ascontiguousarray(np.asarray(inputs["conv_w"], np.float32)),
        "conv_b": np.ascontiguousarray(np.asarray(inputs["conv_b"], np.float32)),
        "dt_bias": np.ascontiguousarray(np.asarray(inputs["dt_bias"], np.float32)),
        "a_log": np.ascontiguousarray(np.asarray(inputs["a_log"], np.float32)),
        "d_skip": np.ascontiguousarray(np.asarray(inputs["d_skip"], np.float32)),
        "ssm_norm": np.ascontiguousarray(np.asarray(inputs["ssm_norm"], np.float32)),
        "norms": norms,
    }
    maps = []
    for i in range(n_cores):
        m = dict(common)
        m["x"] = np.ascontiguousarray(x[i * nseq:(i + 1) * nseq])
        m["pos"] = np.ascontiguousarray(pos[i * nseq:(i + 1) * nseq])
        maps.append(m)
    return maps


def kernel(**inputs):
    nseq = 4
    nc = build(nseq)
    in_maps = make_in_maps(inputs, N_CORES, nseq)
    res = run_bass_kernel_spmd(nc, in_maps, core_ids=list(range(N_CORES)))
    out = np.concatenate([np.asarray(r["out"], dtype=np.float32) for r in res.results], axis=0)
    return out
```
